# Optimizing a Trainium2 kernel written in Bass

```python
import jax, jax.numpy as jnp
from jax import lax
import numpy as np

D_MODEL = 1024
BATCH = 4
SEQ = 4096
DEPTH = 2
DEC_BATCH = 32
DEC_SEQ = 16
PAST_LEN = 1024

CHUNK = 64
N_MIXERS = 2
N_RET_LAYERS = (DEPTH + 1) // 2
N_SGU_LAYERS = DEPTH // 2
RET_HEADS = 4
RET_DK = D_MODEL // RET_HEADS
RET_DV = 2 * RET_DK
RET_QK = RET_HEADS * RET_DK
RET_V = RET_HEADS * RET_DV
ROPE_BASE = 10000.0
SGU_CHUNK = 128
SGU_GROUPS = 4
SGU_D = 3 * D_MODEL
SGU_DG = SGU_D // SGU_GROUPS
FFN_D = 2816
CONV_W = 3
EPS = 1e-6

kernel_name = "hybrid_retention_sgu_convffn_stream_step"


def _rmsnorm(x, g):
    xf = x.astype(jnp.float32)
    y = xf * lax.rsqrt(jnp.mean(xf * xf, axis=-1, keepdims=True) + EPS)
    return (y * g.astype(jnp.float32)).astype(x.dtype)


def _modulate(x, g, shift, scale):
    return _rmsnorm(x, g) * (1 + scale[:, None, :]) + shift[:, None, :]


def _rotary(x, pos):
    half = x.shape[-1] // 2
    inv = jnp.power(ROPE_BASE, -jnp.arange(half, dtype=jnp.float32) / half)
    ang = pos[:, None] * inv[None, :]
    cos = jnp.cos(ang)[None, :, None, :]
    sin = jnp.sin(ang)[None, :, None, :]
    x1, x2 = x[..., :half], x[..., half:]
    return jnp.concatenate([x1 * cos - x2 * sin, x1 * sin + x2 * cos], axis=-1)


def _retention_block(S, qkv, log_gamma):
    q, k, v = qkv
    L = q.shape[1]
    idx = jnp.arange(L, dtype=jnp.float32)
    dist = jnp.abs(idx[:, None] - idx[None, :])
    intra = jnp.exp(log_gamma[:, None, None] * dist[None])
    scores = jnp.einsum('blhd,bshd->bhls', q, k) * intra[None]
    o = jnp.einsum('bhls,bshe->blhe', scores, v)
    q_dec = jnp.exp((idx[:, None] + 1.0) * log_gamma[None, :])
    o = o + jnp.einsum('blhd,bhde->blhe', q * q_dec[None, :, :, None], S)
    k_dec = jnp.exp((L - 1.0 - idx)[:, None] * log_gamma[None, :])
    S = S * jnp.exp(L * log_gamma)[None, :, None, None] + jnp.einsum(
        'blhd,blhe->bhde', k * k_dec[None, :, :, None], v)
    return S, o


def _retention(h, S0, pos0, w_in, gn_g, w_out):
    B, T, _ = h.shape
    proj = h @ w_in
    q = proj[..., :RET_QK].reshape(B, T, RET_HEADS, RET_DK).astype(jnp.float32)
    k = proj[..., RET_QK:2 * RET_QK].reshape(B, T, RET_HEADS, RET_DK).astype(jnp.float32)
    v = proj[..., 2 * RET_QK:2 * RET_QK + RET_V].reshape(B, T, RET_HEADS, RET_DV).astype(jnp.float32)
    gate = proj[..., 2 * RET_QK + RET_V:]
    pos = pos0 + jnp.arange(T, dtype=jnp.float32)
    q = _rotary(q, pos) * (RET_DK ** -0.5)
    k = _rotary(k, pos)
    L = min(T, CHUNK)
    n = T // L

    def to_blocks(a):
        return a.reshape(B, n, L, *a.shape[2:]).swapaxes(0, 1)

    log_gamma = jnp.log(1.0 - jnp.exp2(-5.0 - jnp.arange(RET_HEADS, dtype=jnp.float32)))
    S, o = lax.scan(lambda s, xs: _retention_block(s, xs, log_gamma),
                    S0.astype(jnp.float32), (to_blocks(q), to_blocks(k), to_blocks(v)))
    o = o.swapaxes(0, 1).reshape(B, T, RET_HEADS, RET_DV)
    mu = jnp.mean(o, axis=-1, keepdims=True)
    var = jnp.mean(jnp.square(o - mu), axis=-1, keepdims=True)
    o = ((o - mu) * lax.rsqrt(var + EPS)).reshape(B, T, RET_V) * gn_g.astype(jnp.float32)
    y = (jax.nn.silu(gate) * o.astype(h.dtype)) @ w_out
    return y, S


def _sgu(h, w_in, ln_g, ln_b, w_s, b_s, w_out):
    B, T, _ = h.shape
    z = jax.nn.gelu(h @ w_in)
    u, v = z[..., :SGU_D], z[..., SGU_D:]
    vf = v.astype(jnp.float32)
    mu = jnp.mean(vf, axis=-1, keepdims=True)
    var = jnp.mean(jnp.square(vf - mu), axis=-1, keepdims=True)
    v = ((vf - mu) * lax.rsqrt(var + EPS) * ln_g + ln_b).astype(h.dtype)
    L = min(T, SGU_CHUNK)
    n = T // L
    blk = jnp.arange(SGU_CHUNK) // CHUNK
    w = jnp.where(blk[None, :] <= blk[:, None], w_s, 0)[:, :L, :L]
    mixed = jnp.einsum('gps,bnsgc->bnpgc', w, v.reshape(B, n, L, SGU_GROUPS, SGU_DG))
    mixed = mixed + b_s[:, :L].T[None, None, :, :, None]
    y = (u * mixed.reshape(B, T, SGU_D)) @ w_out
    return y, v


def _conv_ffn(h, buf, w_up, conv_w, conv_b, w_down):
    a = h @ w_up
    T = a.shape[1]
    ap = jnp.concatenate([buf.astype(a.dtype), a], axis=1)
    c = conv_b
    for j in range(CONV_W):
        c = c + ap[:, j:j + T] * conv_w[j]
    gate, val = c[..., :FFN_D], c[..., FFN_D:]
    return (jax.nn.silu(gate) * val) @ w_down, ap[:, -(CONV_W - 1):]


def setup_inputs(seed: int = 0) -> dict:
    key = jax.random.key(seed)
    ks = iter(jax.random.split(key, 32))
    D = D_MODEL

    def nrm(shape, s):
        return s * jax.random.normal(next(ks), shape, jnp.float32)

    return {
        "x_prompt": nrm((BATCH, SEQ, D), 1.0),
        "x_sample": nrm((DEC_BATCH, DEC_SEQ, D), 1.0),
        "state_ret": nrm((N_RET_LAYERS, DEC_BATCH, RET_HEADS, RET_DK, RET_DV), 4.0),
        "state_ffn_conv": nrm((DEPTH, DEC_BATCH, CONV_W - 1, 2 * FFN_D), 1.0),
        "c_prompt": nrm((BATCH, D), 1.0),
        "c_sample": nrm((DEC_BATCH, D), 1.0),
        "w_ada": nrm((DEPTH, D, 6 * D), 0.5 * D ** -0.5),
        "b_ada": nrm((DEPTH, 6 * D), 0.02),
        "norm_mix_g": 1.0 + nrm((DEPTH, D), 0.02),
        "norm_ffn_g": 1.0 + nrm((DEPTH, D), 0.02),
        "ret_w_in": nrm((N_RET_LAYERS, D, 2 * RET_QK + 2 * RET_V), D ** -0.5),
        "ret_gn_g": 1.0 + nrm((N_RET_LAYERS, RET_V), 0.02),
        "ret_w_out": nrm((N_RET_LAYERS, RET_V, D), RET_V ** -0.5),
        "sgu_w_in": nrm((N_SGU_LAYERS, D, 2 * SGU_D), D ** -0.5),
        "sgu_ln_g": 1.0 + nrm((N_SGU_LAYERS, SGU_D), 0.02),
        "sgu_ln_b": nrm((N_SGU_LAYERS, SGU_D), 0.02),
        "sgu_w_s": nrm((N_SGU_LAYERS, SGU_GROUPS, SGU_CHUNK, SGU_CHUNK), 0.5 * SGU_CHUNK ** -0.5),
        "sgu_b_s": 1.0 + nrm((N_SGU_LAYERS, SGU_GROUPS, SGU_CHUNK), 0.02),
        "sgu_w_out": nrm((N_SGU_LAYERS, SGU_D, D), SGU_D ** -0.5),
        "ffn_w_up": nrm((DEPTH, D, 2 * FFN_D), D ** -0.5),
        "ffn_conv_w": nrm((DEPTH, CONV_W, 2 * FFN_D), CONV_W ** -0.5),
        "ffn_conv_b": nrm((DEPTH, 2 * FFN_D), 0.02),
        "ffn_w_down": nrm((DEPTH, FFN_D, D), FFN_D ** -0.5),
        "final_g": 1.0 + nrm((D,), 0.02),
    }


def reference(x_prompt, x_sample, state_ret, state_ffn_conv, c_prompt, c_sample,
              w_ada, b_ada, norm_mix_g, norm_ffn_g,
              ret_w_in, ret_gn_g, ret_w_out,
              sgu_w_in, sgu_ln_g, sgu_ln_b, sgu_w_s, sgu_b_s, sgu_w_out,
              ffn_w_up, ffn_conv_w, ffn_conv_b, ffn_w_down, final_g):
    xp, xs = x_prompt, x_sample
    Bp = xp.shape[0]
    ret_p, ret_s, conv_p, conv_s, sgu_s = [], [], [], [], []
    for i in range(DEPTH):
        mod_p = jax.nn.silu(c_prompt) @ w_ada[i] + b_ada[i]
        mod_s = jax.nn.silu(c_sample) @ w_ada[i] + b_ada[i]
        sh1p, sc1p, g1p, sh2p, sc2p, g2p = jnp.split(mod_p, 6, axis=-1)
        sh1s, sc1s, g1s, sh2s, sc2s, g2s = jnp.split(mod_s, 6, axis=-1)
        hp = _modulate(xp, norm_mix_g[i], sh1p, sc1p)
        hs = _modulate(xs, norm_mix_g[i], sh1s, sc1s)
        j = i // N_MIXERS
        if i % N_MIXERS == 0:
            zero_state = jnp.zeros((Bp, RET_HEADS, RET_DK, RET_DV), jnp.float32)
            op, Sp = _retention(hp, zero_state, 0, ret_w_in[j], ret_gn_g[j], ret_w_out[j])
            os_, Ss = _retention(hs, state_ret[j], PAST_LEN, ret_w_in[j], ret_gn_g[j], ret_w_out[j])
            ret_p.append(Sp.astype(xp.dtype))
            ret_s.append(Ss.astype(state_ret.dtype))
        else:
            op, _ = _sgu(hp, sgu_w_in[j], sgu_ln_g[j], sgu_ln_b[j], sgu_w_s[j], sgu_b_s[j], sgu_w_out[j])
            os_, vs = _sgu(hs, sgu_w_in[j], sgu_ln_g[j], sgu_ln_b[j], sgu_w_s[j], sgu_b_s[j], sgu_w_out[j])
            sgu_s.append(vs)
        xp = xp + g1p[:, None, :] * op
        xs = xs + g1s[:, None, :] * os_
        hp = _modulate(xp, norm_ffn_g[i], sh2p, sc2p)
        hs = _modulate(xs, norm_ffn_g[i], sh2s, sc2s)
        zero_buf = jnp.zeros((Bp, CONV_W - 1, 2 * FFN_D), hp.dtype)
        fp, bp = _conv_ffn(hp, zero_buf, ffn_w_up[i], ffn_conv_w[i], ffn_conv_b[i], ffn_w_down[i])
        fs, bs = _conv_ffn(hs, state_ffn_conv[i], ffn_w_up[i], ffn_conv_w[i], ffn_conv_b[i], ffn_w_down[i])
        conv_p.append(bp)
        conv_s.append(bs)
        xp = xp + g2p[:, None, :] * fp
        xs = xs + g2s[:, None, :] * fs
    y_prompt = _rmsnorm(xp, final_g)
    y_sample = _rmsnorm(xs, final_g)
    return (y_prompt, y_sample, jnp.stack(ret_p), jnp.stack(ret_s),
            jnp.stack(conv_p), jnp.stack(conv_s), jnp.stack(sgu_s))
```

```python
from contextlib import ExitStack
import numpy as np
import ml_dtypes
import concourse.bass as bass
import concourse.mybir as mybir
from concourse.bass_utils import run_bass_kernel_spmd

F32 = mybir.dt.float32
BF16 = mybir.dt.bfloat16
ALU = mybir.AluOpType
AF = mybir.ActivationFunctionType

ENGS = ("pe", "act", "dve", "pool", "sp")
SEM_LIMIT = 30000
DMA_POOL = 48
SBUF_LIMIT = 229344
SBUF_BASE = 16512

D = 1024
HEADS = 4
DK = 256
DV = 512
FFN = 2816
SGUD = 3072
EPS = 1e-6
NPRE = 1792
NR = 2304
NS = 64
NCOL = NR + NS
WMAX = 832
NSLOT = 10
STOP = 10 ** 9
DBG = set()


class Op:
    __slots__ = ("eng", "fn", "deps", "sig", "token", "is_dma", "out_dma")

    def __init__(self, eng, fn, is_dma):
        self.eng = eng
        self.fn = fn
        self.deps = []
        self.sig = False
        self.token = None
        self.is_dma = is_dma
        self.out_dma = False


class Prog:
    def __init__(self):
        self.ops = {e: [] for e in ENGS}
        self.all = []
        self.lastw = {}
        self.readers = {}
        self.dmas = []
        self.dmas_q = {"hw": [], "sw": []}
        self.bar = []
        self.bar_seen = set(ENGS)
        self.dma_since = []

    def barrier(self):
        lasts = [self.ops[e][-1] for e in ENGS if self.ops[e] and e != "pool"]
        pc = [o for o in self.ops["pool"] if not o.is_dma]
        if pc:
            lasts.append(pc[-1])
        self.bar = lasts + list(self.dma_since)
        self.dma_since = []
        self.bar_seen = {"pool"}

    def add(self, eng, fn, reads=(), writes=(), dma=False, out_dma=False):
        op = Op(eng, fn, dma)
        op.out_dma = out_dma
        deps = []
        if eng not in self.bar_seen:
            deps.extend(self.bar)
            self.bar_seen.add(eng)
        for r in reads:
            w = self.lastw.get(r)
            if w is not None:
                deps.append(w)
        for r in writes:
            w = self.lastw.get(r)
            if w is not None:
                deps.append(w)
            deps.extend(self.readers.get(r, ()))
        for r in reads:
            self.readers.setdefault(r, []).append(op)
        for r in writes:
            self.lastw[r] = op
            self.readers[r] = []
        if dma:
            lst = self.dmas_q["sw" if eng == "pool" else "hw"]
            j = len(lst)
            if j >= DMA_POOL // 2:
                deps.append(lst[j - DMA_POOL // 2])
            lst.append(op)
            self.dmas.append(op)
            if eng != "pool":
                self.dma_since.append(op)
        seen = set()
        for d in deps:
            if d is op or id(d) in seen:
                continue
            seen.add(id(d))
            if d.eng == "pe" and eng == "pe" and not d.is_dma:
                continue
            op.deps.append(d)
        self.ops[eng].append(op)
        self.all.append(op)
        return op

    def finish(self, eng="sp"):
        op = Op(eng, None, False)
        op.deps = [d for d in self.dmas if d.out_dma]
        self.ops[eng].append(op)
        self.all.append(op)

    def emit(self, nc):
        for op in self.all:
            for d in op.deps:
                d.sig = True
        with ExitStack() as st:
            sems = {}
            for e in ENGS:
                n = sum(1 for o in self.ops[e] if o.sig and not o.is_dma)
                k = max(1, (n + SEM_LIMIT - 1) // SEM_LIMIT)
                sems[e] = [st.enter_context(nc.semaphore(f"pg_{e}{i}")) for i in range(k)]
            hp = DMA_POOL // 2
            dsem = {q: [st.enter_context(nc.semaphore(f"d{q}{i}")) for i in range(hp)] for q in ("hw", "sw")}
            for e in ENGS:
                c = 0
                for o in self.ops[e]:
                    if o.is_dma or not o.sig:
                        continue
                    k = c // SEM_LIMIT
                    o.token = (sems[e][k], c - k * SEM_LIMIT + 1)
                    c += 1
            for q in ("hw", "sw"):
                for j, o in enumerate(self.dmas_q[q]):
                    o.token = (dsem[q][j % hp], 16 * (j // hp + 1))
            block = st.enter_context(nc.Block())

            def run(e):
                def body(eng):
                    seen = {}
                    for o in self.ops[e]:
                        for d in o.deps:
                            s, v = d.token
                            if seen.get(id(s), 0) >= v:
                                continue
                            eng.wait_ge(s, v)
                            seen[id(s)] = v
                        if o.fn is None:
                            continue
                        ins = o.fn(eng)
                        if o.is_dma:
                            ins.then_inc(o.token[0], 16)
                        elif o.sig:
                            ins.then_inc(o.token[0], 1)

                return body

            block.tensor(run("pe"))
            block.scalar(run("act"))
            block.vector(run("dve"))
            block.gpsimd(run("pool"))
            block.sync(run("sp"))


class Alloc:
    def __init__(self, nc):
        self.nc = nc
        self.off = SBUF_BASE
        self.n = 0
        self.peak = 0

    def __call__(self, shape, dtype):
        nb = 1
        for s in shape[1:]:
            nb *= s
        nb *= 4 if dtype == F32 else 2
        off = (self.off + 31) // 32 * 32
        t = self.nc.alloc_sbuf_tensor_at(f"t{self.n}", list(shape), dtype, offset=off)
        self.n += 1
        self.off = off + nb
        self.peak = max(self.peak, self.off)
        assert self.off <= SBUF_LIMIT, f"SBUF overflow {self.off}"
        return t.ap()

    def mark(self):
        return self.off

    def reset(self, m):
        self.off = m


def build_program():
    nc = bass.Bass("TRN2", target_bir_lowering=False)
    P = Prog()
    A = Alloc(nc)

    def din(name, shape):
        return nc.dram_tensor(name, list(shape), F32, kind="ExternalInput").ap()

    def dout(name, shape):
        return nc.dram_tensor(name, list(shape), F32, kind="ExternalOutput").ap()

    xT_d = din("xT", (D, NCOL))
    xpre_d = din("xpre", (D, NPRE))
    cT_d = din("cT", (128, 40))
    sret_d = din("sret", (4, HEADS, DK, DV))
    sconv_d = din("sconvT", (2, 128, 352))
    cs_d = din("cs", (128, 2, NCOL))
    cspre_d = din("cspre", (128, 2, NPRE))
    tabs_d = din("tabs", (128, 1200))
    identd_d = din("identd", (128, 128))
    gb_d = din("gb", (128, 2, SGUD))
    vec_d = din("vecs", (128, 512))
    w_ada_d = din("w_ada", (2, D, 6 * D))
    ret_w_in_d = din("ret_w_in", (D, 6 * D))
    ret_w_out_d = din("ret_w_out", (2 * D, D))
    sgu_w_in_d = din("sgu_w_in", (D, 2 * SGUD))
    sgu_w_out_d = din("sgu_w_out", (SGUD, D))
    sgu_wsT_d = din("sgu_wsT", (128, 4, 128))
    sgu_bs_d = din("sgu_bs", (1, 4, 128))
    ffn_w_up_d = din("ffn_w_up", (2, D, 2 * FFN))
    ffn_w_down_d = din("ffn_w_down", (2, FFN, D))

    yT_d = dout("yT", (D, NCOL))
    retp_d = dout("retp", (HEADS, DK, DV))
    rets_d = dout("rets", (4, HEADS, DK, DV))
    convp_d = dout("convp", (2, 128, 88))
    convs_d = dout("convs", (2, 128, 352))
    sguv_d = dout("sguv", (NS, SGUD))

    psb = [nc.alloc_psum_tensor(f"ps{i}", [128, 512], F32).ap() for i in range(6)]
    pT = nc.alloc_psum_tensor("pT", [128, 1024], BF16).ap()
    pT2 = nc.alloc_psum_tensor("pT2", [128, 1024], BF16).ap()
    ps_ctr = [0]

    def psum():
        i = ps_ctr[0] % 6
        ps_ctr[0] += 1
        return psb[i], ("ps", i)

    def mm(out, lhsT, rhs, start, stop, reads, writes):
        P.add("pe", lambda e: e.matmul(out, lhsT=lhsT, rhs=rhs, start=start, stop=stop), reads, writes)

    def tr(out, in_, ident, reads, writes):
        P.add("pe", lambda e: e.transpose(out=out, in_=in_, identity=ident), reads, writes)

    def act(out, in_, func, reads, writes, scale=1.0, bias=0.0):
        if func == AF.Copy and not (isinstance(scale, float) and isinstance(bias, float)):
            func = AF.Identity
        P.add("act", lambda e: e.activation(out=out, in_=in_, func=func, bias=bias, scale=scale), reads, writes)

    def tt(eng, out, in0, in1, op, reads, writes):
        P.add(eng, lambda e: e.tensor_tensor(out=out, in0=in0, in1=in1, op=op), reads, writes)

    def ts(eng, out, in0, s1, s2, op0, op1, reads, writes):
        if s2 is None:
            P.add(eng, lambda e: e.tensor_scalar(out=out, in0=in0, scalar1=s1, scalar2=None, op0=op0), reads, writes)
        else:
            P.add(eng, lambda e: e.tensor_scalar(out=out, in0=in0, scalar1=s1, scalar2=s2, op0=op0, op1=op1), reads, writes)

    def stt(eng, out, in0, scalar, in1, op0, op1, reads, writes):
        P.add(eng, lambda e: e.scalar_tensor_tensor(out=out, in0=in0, scalar=scalar, in1=in1, op0=op0, op1=op1), reads, writes)

    def cp(eng, out, in_, reads, writes):
        if eng == "act":
            act(out, in_, AF.Copy, reads, writes)
        else:
            P.add(eng, lambda e: e.tensor_copy(out=out, in_=in_), reads, writes)

    def dma(q, out, in_, reads=(), writes=(), out_dma=False):
        P.add(q, lambda e: e.dma_start(out=out, in_=in_), reads, writes, dma=True, out_dma=out_dma)

    def wdma(out3, in3, wn, nk, base=0):
        step = 8
        names = []
        for k0 in range(0, nk, step):
            k1 = min(nk, k0 + step)
            nm_ = (wn, base + k0 // step)
            dma("pool", out3[:, k0:k1, :], in3[:, k0:k1, :], writes=[nm_])
            names.append(nm_)
        return names

    def memset(eng, ap, val, writes):
        P.add(eng, lambda e: e.memset(ap, val), (), writes)

    x = A([128, 8, WMAX], F32)
    hT = A([128, 8, WMAX], BF16)
    rstd = A([128, WMAX], F32)
    Sst = A([128, HEADS, 2, DV], F32)
    tabs = A([128, 1200], F32)
    vecs = A([128, 512], F32)
    mod = A([128, 2, 48, 5], F32)
    mul1 = A([128, 2, 2, 8, 5], F32)
    ident = A([128, 128], BF16)
    ones = A([128, 128], BF16)
    flag = tabs[:, 1196:1197]
    hal = A([128, 2, 44, 2], F32)
    n_sq = A([128, 8, 384], BF16)
    n_xr = [A([128, 384], F32) for _ in range(2)]
    NWS = 3
    wsl = [A([128, 6144], BF16) for _ in range(NWS)]
    ws_ctr = [0]

    def wslot():
        i = ws_ctr[0] % NWS
        ws_ctr[0] += 1
        return wsl[i], ("w", i)

    maskT = tabs[:, 0:512].rearrange("p (h l) -> p h l", h=4)
    mask16 = tabs[:, 512:576].rearrange("p (h l) -> p h l", h=4)
    qd = tabs[:, 576:1088].rearrange("p (h l) -> p h l", h=4)
    qd16 = tabs[:, 1088:1152].rearrange("p (h l) -> p h l", h=4)
    kd_main = tabs[:, 1152:1156]
    kd_halo = tabs[:, 1156:1160]
    kd16 = tabs[:, 1160:1164]
    identf = None
    nmg = vecs[:, 0:16].rearrange("p (l c) -> p l c", l=2)
    nfg = vecs[:, 16:32].rearrange("p (l c) -> p l c", l=2)
    fing = vecs[:, 32:40]
    gng = vecs[:, 40:56]
    bada = vecs[:, 56:152].rearrange("p (l c) -> p l c", l=2)
    cw = vecs[:, 152:416].rearrange("p (l c t) -> p l c t", l=2, t=3)
    cb = vecs[:, 416:504].rearrange("p (l c) -> p l c", l=2)

    GAM = [1.0 - 2.0 ** (-5.0 - h) for h in range(HEADS)]

    dma("sp", tabs, tabs_d, writes=["tabs"])
    dma("sp", vecs, vec_d, writes=["vecs"])
    memset("dve", hal, 0.0, [("hal", l_, c_) for l_ in range(2) for c_ in range(44)])
    memset("dve", Sst, 0.0, [("S", 0), ("S", 1)])
    memset("dve", ones, 1.0, ["ones"])

    cTb = A([128, 8, 5], BF16)
    arena0 = A.mark()

    idf = A([128, 128], F32)
    cTs = A([128, 8, 5], F32)
    dma("sp", idf, identd_d, writes=["idf"])
    cp("dve", ident, idf, ["idf"], ["ident"])
    dma("sp", cTs, cT_d.rearrange("p (c s) -> p c s", c=8), writes=["cTs"])
    act(cTb, cTs, AF.Silu, ["cTs"], ["cTb"])
    def adaln_part(l, pcs):
        for pc in pcs:
            wt, wn = wslot()
            wv = wt[:, 0:4096].rearrange("p (k n) -> p k n", k=8)
            wns = wdma(wv, w_ada_d[l, :, pc * 512:(pc + 1) * 512].rearrange("(k p) n -> p k n", p=128), wn, 8)
            ps, pn = psum()
            for oc in range(4):
                for kc in range(8):
                    mm(ps[:, oc * 8:oc * 8 + 5], wv[:, kc, oc * 128:(oc + 1) * 128], cTb[:, kc, :], kc == 0, kc == 7,
                       wns + ["cTb"], [pn])
            for oc in range(4):
                ch = pc * 4 + oc
                act(mod[:, l, ch, :], ps[:, oc * 8:oc * 8 + 5], AF.Identity, [pn, "vecs"], ["mod"], bias=bada[:, l, ch:ch + 1])

    def mul1_part(l, j):
        gv, sc0 = ((nmg, 8), (nfg, 32))[j]
        for c in range(8):
            ts("dve", mul1[:, l, j, c, :], mod[:, l, sc0 + c, :], 1.0, gv[:, l, c:c + 1], ALU.add, ALU.mult,
               ["mod", "vecs"], ["mul1"])

    adaln_part(0, range(0, 4))
    mul1_part(0, 0)
    ada_todo = [(0, pc) for pc in range(4, 12)] + [(1, pc) for pc in range(12)]

    def ada_step(k):
        for _ in range(k):
            if ada_todo:
                l_, pc_ = ada_todo.pop(0)
                adaln_part(l_, [pc_])
                if not ada_todo:
                    mul1_part(0, 1)
                    mul1_part(1, 0)
                    mul1_part(1, 1)
    P.barrier()
    A.reset(arena0)

    def make_st(mode, src, c0, nblk, halo=0, sample=False):
        blocks = []
        for b in range(nblk):
            blocks.append(dict(c0=b * 128, rows=128, samp=None, kd=(kd_halo if (mode == "pre" or b < halo) else kd_main)))
        W = nblk * 128
        tiles = []
        cc = 0
        while cc < W:
            n = min(384, W - cc)
            tiles.append((cc, n, [i for i in range(nblk) if cc <= i * 128 < cc + n]))
            cc += n
        seqs = [(0, W, 0)]
        if sample:
            for s in range(4):
                blocks.append(dict(c0=W + 16 * s, rows=16, samp=s, kd=kd16))
                seqs.append((W + 16 * s, 16, 1 + s))
            tiles.append((W, 64, [nblk + s for s in range(4)]))
            W += 64
        return dict(mode=mode, src=src, c0=c0, blocks=blocks, tiles=tiles, seqs=seqs, W=W, Wp=nblk * 128,
                    sample=sample, halo=halo)

    STS = [
        make_st("pre", xpre_d, 0, 6), make_st("pre", xpre_d, 768, 6), make_st("pre", xpre_d, 1536, 2),
        make_st("full", xT_d, 0, 6, halo=2), make_st("full", xT_d, 768, 6), make_st("full", xT_d, 1536, 6, sample=True),
    ]

    def norm_mod(st, l, j, final=False):
        W = st["W"]
        m0 = A.mark()
        sqs = [n_sq]
        xrs = n_xr
        xi = [0]
        sd = rstd
        for ti_, (c0, n, _) in enumerate(st["tiles"]):
            sq = sqs[0]
            sqn = ("sq", 0)
            for kc in range(8):
                act(sq[:, kc, 0:n], x[:, kc, c0:c0 + n], AF.Square, [("x", kc)], [sqn])
            ps, pn = psum()
            for kc in range(8):
                mm(ps[:, 0:n], ones, sq[:, kc, 0:n], kc == 0, kc == 7, ["ones", sqn], [pn])
            act(sd[:, c0:c0 + n], ps[:, 0:n], AF.Copy, [pn], ["rstd"])
        ts("dve", sd[:, 0:W], sd[:, 0:W], 1.0 / D, EPS, ALU.mult, ALU.add, ["rstd"], ["rstd"])
        act(sd[:, 0:W], sd[:, 0:W], AF.Sqrt, ["rstd"], ["rstd"])
        P.add("dve", lambda e: e.reciprocal(out=rstd[:, 0:W], in_=sd[:, 0:W]), ["rstd"], ["rstd"])
        if final:
            yo = A([128, 8, WMAX], F32)
            for kc in range(8):
                stt("dve", yo[:, kc, 0:W], x[:, kc, 0:W], fing[:, kc:kc + 1], rstd[:, 0:W], ALU.mult, ALU.mult,
                    [("x", kc), "rstd", "vecs"], [("yo", kc)])
                dma("sp", yT_d[kc * 128:(kc + 1) * 128, st["c0"]:st["c0"] + W], yo[:, kc, 0:W], reads=[("yo", kc)], out_dma=True)
        else:
            sh0 = 0 if j == 0 else 24
            for kc in range(8):
                for (c0, n, sq_) in st["seqs"]:
                    for cc in range(c0, c0 + n, 384):
                        nn = min(384, c0 + n - cc)
                        xr = xrs[xi[0] % 2]
                        xrn = ("xr", xi[0] % 2)
                        xi[0] += 1
                        stt("dve", xr[:, 0:nn], x[:, kc, cc:cc + nn], mul1[:, l, j, kc, sq_:sq_ + 1], rstd[:, cc:cc + nn],
                            ALU.mult, ALU.mult, [("x", kc), "rstd", "mul1"], [xrn])
                        act(hT[:, kc, cc:cc + nn], xr[:, 0:nn], AF.Identity, [xrn, "mod"], [hn(st, kc, cc)],
                            bias=mod[:, l, sh0 + kc, sq_:sq_ + 1])
        A.reset(m0)

    def hn(st, kc, col):
        return ("hT", kc, col // 384 if col < st["Wp"] else 9)

    def resid_evac(st, ps, pn, oc, c0, n, l, gch, reads):
        for (s0, sn, sq_) in st["seqs"]:
            a = max(s0, c0)
            b = min(s0 + sn, c0 + n)
            if a >= b:
                continue
            stt("dve", x[:, oc, a:b], ps[:, a - c0:b - c0], mod[:, l, gch + oc, sq_:sq_ + 1], x[:, oc, a:b],
                ALU.mult, ALU.add, [pn, "mod", ("x", oc)] + reads, [("x", oc)])

    def load_x(st):
        W = st["W"]
        for kc in range(8):
            dma("sp", x[:, kc, 0:W], st["src"][kc * 128:(kc + 1) * 128, st["c0"]:st["c0"] + W], writes=[("x", kc)])

    def retention(st, cs_all, cs_c0):
        pre = st["mode"] == "pre"
        W = st["W"]
        blocks = st["blocks"]
        m0 = A.mark()
        cs = A([128, 2, WMAX], F32)
        dma("sp", cs[:, :, 0:W], cs_all[:, :, cs_c0:cs_c0 + W], writes=["cs"])
        kT = A([128, 2, WMAX], BF16)
        vtok = A([128, NSLOT, DV], BF16)
        kdt = A([128, NSLOT, DK], BF16)
        rts = [[A([128, 384], F32) for _ in range(4)] for _ in range(2)]
        rti = [0]
        Sbf = A([128, 2, DV], BF16)
        if not pre:
            qT = A([128, 2, WMAX], BF16)
            qdT = A([128, 2, WMAX], BF16)
            sgT = A([128, 4, WMAX], BF16)
            yT = A([128, 16, WMAX], BF16)
            psb16_l = [A([128, 128], BF16) for _ in range(2)]
            onorm_l = [A([128, DV], BF16) for _ in range(2)]
            bst_l = [A([128, 6], F32) for _ in range(2)]
            bmv_l = [A([128, 2], F32) for _ in range(2)]
            brs_l = [A([128, 1], F32) for _ in range(2)]
            bnb_l = [A([128, 1], F32) for _ in range(2)]
            if st["sample"]:
                S0f = A([128, 2, DV], F32)
                S0b = A([128, 2, DV], BF16)
                Sof = A([128, 2, DV], F32)

        def rotary(p1, p2, pn1, pn2, dst, c0, n, rname):
            cosv = cs[:, 0, c0:c0 + n]
            sinv = cs[:, 1, c0:c0 + n]
            k_ = rti[0] % 2
            rti[0] += 1
            rt = rts[k_]
            tt("dve", rt[0][:, 0:n], p1[:, 0:n], cosv, ALU.mult, [pn1, "cs"], [("rt0", k_)])
            tt("dve", rt[1][:, 0:n], p2[:, 0:n], sinv, ALU.mult, [pn2, "cs"], [("rt1", k_)])
            tt("dve", rt[2][:, 0:n], p1[:, 0:n], sinv, ALU.mult, [pn1, "cs"], [("rt2", k_)])
            tt("dve", rt[3][:, 0:n], p2[:, 0:n], cosv, ALU.mult, [pn2, "cs"], [("rt3", k_)])
            tt("dve", dst[:, 0, c0:c0 + n], rt[0][:, 0:n], rt[1][:, 0:n], ALU.subtract, [("rt0", k_), ("rt1", k_)], [rname])
            tt("dve", dst[:, 1, c0:c0 + n], rt[2][:, 0:n], rt[3][:, 0:n], ALU.add, [("rt2", k_), ("rt3", k_)], [rname])

        for h in range(HEADS):
            wqk, nqk = wslot()
            wqkv = wqk[:, 0:4096].rearrange("p (k n) -> p k n", k=8)
            src = ret_w_in_d.rearrange("(k p) n -> p k n", p=128)
            nqk_l = wdma(wqkv[:, :, 0:256], src[:, :, h * DK:(h + 1) * DK], nqk, 8, 0) + \
                wdma(wqkv[:, :, 256:512], src[:, :, D + h * DK:D + (h + 1) * DK], nqk, 8, 1)
            wv_, nv = wslot()
            wvv = wv_[:, 0:4096].rearrange("p (k n) -> p k n", k=8)
            nv_l = wdma(wvv, src[:, :, 2 * D + h * DV:2 * D + (h + 1) * DV], nv, 8)
            if not pre:
                wg_, ng = wslot()
                wgv = wg_[:, 0:4096].rearrange("p (k n) -> p k n", k=8)
                ng_l = wdma(wgv, src[:, :, 4 * D + h * DV:4 * D + (h + 1) * DV], ng, 8)
            for (c0, n, bl) in st["tiles"]:
                for which in ((1,) if pre else (0, 1)):
                    pp = []
                    for dc in range(2):
                        ps, pn = psum()
                        for kc in range(8):
                            mm(ps[:, 0:n], wqkv[:, kc, which * 256 + dc * 128:which * 256 + (dc + 1) * 128],
                               hT[:, kc, c0:c0 + n], kc == 0, kc == 7, nqk_l + [hn(st, kc, c0)], [pn])
                        pp.append((ps, pn))
                    rotary(pp[0][0], pp[1][0], pp[0][1], pp[1][1], kT if which else qT, c0, n, "kT" if which else "qT")
                for bi in bl:
                    b = blocks[bi]
                    r = b["rows"]
                    ps, pn = psum()
                    for kc in range(8):
                        mm(ps[0:r, :], hT[:, kc, b["c0"]:b["c0"] + r], wvv[:, kc, :], kc == 0, kc == 7, nv_l + [hn(st, kc, b["c0"])], [pn])
                    cp("act", vtok[0:r, bi, :], ps[0:r, :], [pn], [("vtok", bi)])
                if not pre:
                    for ec in range(4):
                        ps, pn = psum()
                        for kc in range(8):
                            mm(ps[:, 0:n], wgv[:, kc, ec * 128:(ec + 1) * 128], hT[:, kc, c0:c0 + n], kc == 0, kc == 7,
                               ng_l + [hn(st, kc, c0)], [pn])
                        act(sgT[:, ec, c0:c0 + n], ps[:, 0:n], AF.Silu, [pn], ["sgT"])
                        ts("dve", sgT[:, ec, c0:c0 + n], sgT[:, ec, c0:c0 + n], gng[:, h * 4 + ec:h * 4 + ec + 1], None, ALU.mult, None,
                           ["sgT", "vecs"], ["sgT"])
            pend_tail = []

            def emit_tail(on_, onn_, r_, c0_, h=h):
                for ec in range(4):
                    tr(pT2[:, ec * 128:ec * 128 + r_], on_[0:r_, ec * 128:(ec + 1) * 128], ident[0:r_, 0:r_],
                       [onn_, "ident"], ["pT2"])
                tt("dve", yT[:, h * 4:(h + 1) * 4, c0_:c0_ + r_], pT2[:, 0:512].rearrange("p (e l) -> p e l", e=4)[:, :, 0:r_],
                   sgT[:, :, c0_:c0_ + r_], ALU.mult, ["pT2", "sgT"], [("yT", h)])

            if not pre:
                for dc_ in range(2):
                    cp("act", Sbf[:, dc_, :], Sst[:, h, dc_, :], [("S", dc_)], [("Sbf", dc_)])
            p_idx = [i for i, b_ in enumerate(blocks) if b_["samp"] is None]
            s_idx = [i for i, b_ in enumerate(blocks) if b_["samp"] is not None]
            order = []
            while p_idx or s_idx:
                if p_idx:
                    order.append(p_idx.pop(0))
                if s_idx:
                    order.append(s_idx.pop(0))
            for oi, bi in enumerate(order):
                b = blocks[bi]
                r = b["rows"]
                c0 = b["c0"]
                samp = b["samp"]
                L = 16 if samp is not None else 128
                if not pre:
                    rb = oi % 2
                    psb16, onorm, bst, bmv, brs, bnb = psb16_l[rb], onorm_l[rb], bst_l[rb], bmv_l[rb], brs_l[rb], bnb_l[rb]
                    nm = lambda base: (base, rb)
                if samp is not None and "noR" in DBG:
                    continue
                for dc in range(2):
                    tr(pT[0:r, dc * 128:(dc + 1) * 128], kT[:, dc, c0:c0 + r], ident, ["kT", "ident"], ["pT"])
                for dc in range(2):
                    act(kdt[0:r, bi, dc * 128:(dc + 1) * 128], pT[0:r, dc * 128:(dc + 1) * 128], AF.Copy, ["pT", "tabs"],
                        [("kdt", bi)], scale=b["kd"][0:r, h:h + 1])
                if samp is not None and "noD" not in DBG:
                    dma("sp", S0f, sret_d[samp, h].rearrange("(c p) e -> p c e", p=128), writes=["S0f"])
                    cp("act", S0b, S0f, ["S0f"], ["S0b"])
                    Sb_, Sbn = S0b, "S0b"
                elif samp is not None:
                    Sb_, Sbn = S0b, "S0b"
                else:
                    Sb_, Sbn = Sbf, "Sbf"
                do_o = not pre and not (samp is not None and "noO" in DBG)
                if do_o:
                    ps, pn = psum()
                    for dc in range(2):
                        mm(ps[0:r, 0:r], kT[:, dc, c0:c0 + r], qT[:, dc, c0:c0 + r], dc == 0, dc == 1, ["kT", "qT"], [pn])
                    mk = mask16[0:r, h, 0:r] if samp is not None else maskT[0:r, h, 0:r]
                    tt("dve", psb16[0:r, 0:r], ps[0:r, 0:r], mk, ALU.mult, [pn, "tabs"], [nm("psb16")])
                    qdv = (qd16 if samp is not None else qd)[:, h, 0:r]
                    for dc in range(2):
                        tt("dve", qdT[:, dc, c0:c0 + r], qT[:, dc, c0:c0 + r], qdv, ALU.mult, ["qT", "tabs"], [("qdT", bi)])
                    po, pon = psum()
                    mm(po[0:r, :], psb16[0:r, 0:r], vtok[0:r, bi, :], True, False, [nm("psb16"), ("vtok", bi)], [pon])
                    for dc in range(2):
                        mm(po[0:r, :], qdT[:, dc, c0:c0 + r], Sb_[:, dc, :], False, dc == 1,
                           [("qdT", bi), (Sbn, dc) if Sbn == "Sbf" else Sbn], [pon])
                if samp is not None and "noS" in DBG:
                    continue
                dec = float(np.float32(GAM[h]) ** np.float32(L))
                for dc in range(2):
                    ps, pn = psum()
                    mm(ps[:, :], kdt[0:r, bi, dc * 128:(dc + 1) * 128], vtok[0:r, bi, :], True, True,
                       [("kdt", bi), ("vtok", bi)], [pn])
                    if samp is not None:
                        stt("dve", Sof[:, dc, :], S0f[:, dc, :], dec, ps[:, :], ALU.mult, ALU.add, ["S0f", pn], ["Sof"])
                    else:
                        stt("dve", Sst[:, h, dc, :], Sst[:, h, dc, :], dec, ps[:, :], ALU.mult, ALU.add, [("S", dc), pn], [("S", dc)])
                        if not pre:
                            cp("act", Sbf[:, dc, :], Sst[:, h, dc, :], [("S", dc)], [("Sbf", dc)])
                if samp is not None and "noD" in DBG:
                    pass
                elif samp is not None:
                    dma("sp", rets_d[samp, h].rearrange("(c p) e -> p c e", p=128), Sof, reads=["Sof"], out_dma=True)
                if do_o:
                    P.add("dve", (lambda o_, i_: (lambda e: e.bn_stats(out=o_, in_=i_)))(bst[0:r, :], po[0:r, :]), [pon], [nm("bst")])
                    P.add("dve", (lambda o_, i_: (lambda e: e.bn_aggr(out=o_, in_=i_)))(bmv[0:r, :], bst[0:r, :]), [nm("bst")], [nm("bmv")])
                    ts("dve", brs[0:r, :], bmv[0:r, 1:2], EPS, None, ALU.add, None, [nm("bmv")], [nm("brs")])
                    act(brs[0:r, :], brs[0:r, :], AF.Sqrt, [nm("brs")], [nm("brs")])
                    P.add("dve", (lambda o_, i_: (lambda e: e.reciprocal(out=o_, in_=i_)))(brs[0:r, :], brs[0:r, :]), [nm("brs")], [nm("brs")])
                    stt("dve", bnb[0:r, :], bmv[0:r, 0:1], -1.0, brs[0:r, 0:1], ALU.mult, ALU.mult, [nm("bmv"), nm("brs")], [nm("bnb")])
                    act(onorm[0:r, :], po[0:r, :], AF.Identity, [pon, nm("brs"), nm("bnb")], [nm("onorm")],
                        scale=brs[0:r, 0:1], bias=bnb[0:r, 0:1])
                    pend_tail.append((onorm, nm("onorm"), r, c0))
                    if len(pend_tail) > 1:
                        emit_tail(*pend_tail.pop(0))
            while pend_tail:
                emit_tail(*pend_tail.pop(0))
            if pre:
                ada_step(2)
        if not pre:
            ada_step(100)
            for pc in range(4):
                wo, no = wslot()
                wov = wo[:, 0:4096].rearrange("p (k n) -> p k n", k=16)
                no_l = wdma(wov, ret_w_out_d[:, pc * 256:(pc + 1) * 256].rearrange("(k p) n -> p k n", p=128), no, 16)
                for o2 in range(2):
                    oc = pc * 2 + o2
                    for (c0, n, _) in st["tiles"]:
                        ps, pn = psum()
                        for kc in range(16):
                            mm(ps[:, 0:n], wov[:, kc, o2 * 128:(o2 + 1) * 128], yT[:, kc, c0:c0 + n], kc == 0, kc == 15,
                               no_l + [("yT", kc // 4)], [pn])
                        resid_evac(st, ps, pn, oc, c0, n, 0, 16, [])
        A.reset(m0)

    def conv_ffn(st, l, last):
        W = st["W"]
        m0 = A.mark()
        mT = A([128, 22, WMAX], BF16)
        ab = [[A([128, 386], F32) for _ in range(2)] for _ in range(2)]
        acc = [[A([128, 384], F32) for _ in range(2)] for _ in range(2)]
        sgb = [A([128, 384], F32) for _ in range(2)]
        if st["sample"]:
            scs = A([128, 44, 4, 2], F32)
            dma("sp", scs, sconv_d[l].rearrange("p (c s r) -> p c s r", c=44, s=4), writes=["scs"])
            abs_ = [A([128, 4, 18], F32) for _ in range(2)]
            accs = [A([128, 4, 16], F32) for _ in range(2)]
            sgs = A([128, 4, 16], F32)
            csts = A([128, 44, 4, 2], F32)
        if last:
            cstp = A([128, 44, 2], F32)
        if st["halo"]:
            ts("dve", hT[:, :, 254:256], hT[:, :, 254:256], flag, None, ALU.mult, None, [("hT", k, 0) for k in range(8)] + ["tabs"],
               [("hT", k, 0) for k in range(8)])
        it = 0
        for pj in range(11):
            wt, wn = wslot()
            wv = wt[:, 0:4096].rearrange("p (k n) -> p k n", k=8)
            src = ffn_w_up_d[l].rearrange("(k p) n -> p k n", p=128)
            wns = wdma(wv[:, :, 0:256], src[:, :, pj * 256:(pj + 1) * 256], wn, 8, 0) + \
                wdma(wv[:, :, 256:512], src[:, :, FFN + pj * 256:FFN + (pj + 1) * 256], wn, 8, 1)
            for jj in range(2):
                j = pj * 2 + jj
                chs = (j, 22 + j)
                for ti, (c0, n, bl) in enumerate(st["tiles"]):
                    is_s = st["sample"] and ti == len(st["tiles"]) - 1
                    pps = []
                    for gv in range(2):
                        ps, pn = psum()
                        for kc in range(8):
                            mm(ps[:, 0:n], wv[:, kc, gv * 256 + jj * 128:gv * 256 + (jj + 1) * 128], hT[:, kc, c0:c0 + n],
                               kc == 0, kc == 7, wns + [hn(st, kc, c0)], [pn])
                        pps.append((ps, pn))
                    bsel = it % 2
                    it += 1
                    nprompt = len(st["tiles"]) - (1 if st["sample"] else 0)
                    for gv in range(2):
                        ps, pn = pps[gv]
                        ch = chs[gv]
                        if not is_s:
                            a_ = ab[bsel][gv]
                            an = ("ab", bsel, gv)
                            ahn = ("abh", bsel, gv)
                            ac_ = acc[bsel][gv]
                            acn = ("acc", bsel, gv)
                            if ti == 0:
                                cp("act", a_[:, 0:2], hal[:, l, ch, :], [("hal", l, ch)], [ahn])
                            act(a_[:, 2:2 + n], ps[:, 0:n], AF.Copy, [pn], [an])
                            if ti + 1 < nprompt:
                                cp("act", ab[1 - bsel][gv][:, 0:2], ps[:, n - 2:n], [pn], [("abh", 1 - bsel, gv)])
                            else:
                                cp("act", hal[:, l, ch, :], ps[:, n - 2:n], [pn], [("hal", l, ch)])
                                if last:
                                    cp("act", cstp[:, ch, :], ps[:, n - 2:n], [pn], ["cstp"])
                            act(ac_[:, 0:n], ps[:, 0:n], AF.Identity, [pn, "vecs"], [acn], scale=cw[:, l, ch, 2:3], bias=cb[:, l, ch:ch + 1])
                            stt("dve", ac_[:, 0:n], a_[:, 1:1 + n], cw[:, l, ch, 1:2], ac_[:, 0:n], ALU.mult, ALU.add, [an, ahn, acn, "vecs"], [acn])
                            stt("dve", ac_[:, 0:n], a_[:, 0:n], cw[:, l, ch, 0:1], ac_[:, 0:n], ALU.mult, ALU.add, [an, ahn, acn, "vecs"], [acn])
                        else:
                            a_ = abs_[gv]
                            an = ("abs", gv)
                            ac_ = accs[gv]
                            acn = ("accs", gv)
                            cp("act", a_[:, :, 0:2], scs[:, ch, :, :], ["scs"], [an])
                            act(a_[:, :, 2:18], ps[:, 0:64].rearrange("p (s t) -> p s t", s=4), AF.Copy, [pn], [an])
                            act(ac_, ps[:, 0:64].rearrange("p (s t) -> p s t", s=4), AF.Identity, [pn, "vecs"], [acn],
                                scale=cw[:, l, ch, 2:3], bias=cb[:, l, ch:ch + 1])
                            stt("dve", ac_, a_[:, :, 1:17], cw[:, l, ch, 1:2], ac_, ALU.mult, ALU.add, [an, acn, "vecs"], [acn])
                            stt("dve", ac_, a_[:, :, 0:16], cw[:, l, ch, 0:1], ac_, ALU.mult, ALU.add, [an, acn, "vecs"], [acn])
                            cp("act", csts[:, ch, :, :], a_[:, :, 16:18], [an], ["csts"])
                    if not is_s:
                        act(sgb[bsel][:, 0:n], acc[bsel][0][:, 0:n], AF.Silu, [("acc", bsel, 0)], [("sgb", bsel)])
                        tt("dve", mT[:, j, c0:c0 + n], sgb[bsel][:, 0:n], acc[bsel][1][:, 0:n], ALU.mult,
                           [("sgb", bsel), ("acc", bsel, 1)], [("mT", j)])
                    else:
                        act(sgs, accs[0], AF.Silu, [("accs", 0)], ["sgs"])
                        tt("dve", mT[:, j, c0:c0 + 64].rearrange("p (s t) -> p s t", s=4), sgs, accs[1], ALU.mult,
                           ["sgs", ("accs", 1)], [("mT", j)])
        if st["sample"]:
            dma("sp", convs_d[l].rearrange("p (c s r) -> p c s r", c=44, s=4), csts, reads=["csts"], out_dma=True)
        if last:
            dma("sp", convp_d[l].rearrange("p (c r) -> p c r", c=44), cstp, reads=["cstp"], out_dma=True)
        for pc in range(4):
            wt, wn = wslot()
            wv = wt[:, 0:5632].rearrange("p (k n) -> p k n", k=22)
            wns = wdma(wv, ffn_w_down_d[l][:, pc * 256:(pc + 1) * 256].rearrange("(k p) n -> p k n", p=128), wn, 22)
            for o2 in range(2):
                oc = pc * 2 + o2
                for (c0, n, _) in st["tiles"]:
                    ps, pn = psum()
                    for kc in range(22):
                        mm(ps[:, 0:n], wv[:, kc, o2 * 128:(o2 + 1) * 128], mT[:, kc, c0:c0 + n], kc == 0, kc == 21,
                           wns + [("mT", kc)], [pn])
                    resid_evac(st, ps, pn, oc, c0, n, l, 40, [])
        A.reset(m0)

    def sgu(st):
        W = st["W"]
        blocks = st["blocks"]
        nb = len(blocks)
        m0 = A.mark()
        vg = A([128, 24, NSLOT * 128], BF16)
        gB = A([128, 2, SGUD], F32)
        dma("sp", gB, gb_d, writes=["gB"])
        wsb = A([128, 4, 128], BF16)
        wsb16 = A([128, 4, 16], BF16)
        bsb = A([1, 4, 128], BF16)
        vt = [A([128, 4, 128], F32) for _ in range(2)]
        ut = [v_.rearrange("p c f -> p (c f)") for v_ in vt]
        wsf = vt[0]
        bsf = vt[1][0:1]
        dma("sp", wsf, sgu_wsT_d, writes=[("vt", 0)])
        dma("sp", bsf, sgu_bs_d, writes=[("vt", 1)])
        cp("dve", wsb16, wsf[:, :, 0:16], [("vt", 0)], ["wsb16"])
        cp("dve", wsb, wsf, [("vt", 0)], ["wsb"])
        memset("dve", wsb[64:128, :, 0:64], 0.0, ["wsb"])
        cp("dve", bsb, bsf, [("vt", 1)], ["bsb"])
        bst = A([128, NSLOT, 6, 6], F32)
        bmv = A([128, NSLOT, 2], F32)
        brs = A([128, NSLOT], F32)
        lt = vt
        memset("dve", bmv, 1.0, ["bmv"])

        src = sgu_w_in_d.rearrange("(k p) n -> p k n", p=128)
        it = 0
        for pv in range(6):
            wt, wn = wslot()
            wv = wt[:, 0:4096].rearrange("p (k n) -> p k n", k=8)
            wns = wdma(wv, src[:, :, SGUD + pv * 512:SGUD + (pv + 1) * 512], wn, 8)
            for bi, b in enumerate(blocks):
                r = b["rows"]
                ps, pn = psum()
                for kc in range(8):
                    mm(ps[0:r, :], hT[:, kc, b["c0"]:b["c0"] + r], wv[:, kc, :], kc == 0, kc == 7, wns + [hn(st, kc, b["c0"])], [pn])
                bs_ = it % 2
                it += 1
                act(vt[bs_][0:r], ps[0:r, :].rearrange("p (c f) -> p c f", c=4), AF.Gelu_apprx_tanh, [pn], [("vt", bs_)])
                P.add("dve", (lambda o_, i_: (lambda e: e.bn_stats(out=o_, in_=i_)))(bst[0:r, bi, pv, :], vt[bs_][0:r].rearrange("p c f -> p (c f)")),
                      [("vt", bs_)], ["bst"])
                cp("dve", vg[0:r, pv * 4:(pv + 1) * 4, bi * 128:(bi + 1) * 128], vt[bs_][0:r], [("vt", bs_)],
                   [("vv", c) for c in range(pv * 4, pv * 4 + 4)])
        for bi, b in enumerate(blocks):
            r = b["rows"]
            P.add("dve", (lambda o_, i_: (lambda e: e.bn_aggr(out=o_, in_=i_)))(bmv[0:r, bi, :], bst[0:r, bi, :, :].rearrange("p a b -> p (a b)")),
                  ["bst"], ["bmv"])
        ts("dve", brs[:, 0:nb], bmv[:, 0:nb, 1], EPS, None, ALU.add, None, ["bmv"], ["brs"])
        act(brs[:, 0:nb], brs[:, 0:nb], AF.Sqrt, ["brs"], ["brs"])
        P.add("dve", lambda e: e.reciprocal(out=brs[:, 0:nb], in_=brs[:, 0:nb]), ["brs"], ["brs"])
        lts = vt
        snb = A([128, NSLOT], F32)
        stt("dve", snb[:, 0:nb], bmv[:, 0:nb, 0], -1.0, brs[:, 0:nb], ALU.mult, ALU.mult, ["bmv", "brs"], ["snb"])
        li = 0
        for bi, b in enumerate(blocks):
            r = b["rows"]
            for pv in range(6):
                lt_ = lts[li % 2]
                ln_ = ("vt", li % 2)
                li += 1
                vsl = vg[0:r, pv * 4:(pv + 1) * 4, bi * 128:(bi + 1) * 128]
                names = [("vv", c) for c in range(pv * 4, pv * 4 + 4)]
                act(lt_[0:r], vsl, AF.Identity, names + ["snb", "brs"], [ln_], scale=brs[0:r, bi:bi + 1], bias=snb[0:r, bi:bi + 1])
                gsl = gB[0:r, 0, pv * 512:(pv + 1) * 512].rearrange("p (c f) -> p c f", c=4)
                bsl = gB[0:r, 1, pv * 512:(pv + 1) * 512].rearrange("p (c f) -> p c f", c=4)
                tt("dve", lt_[0:r], lt_[0:r], gsl, ALU.mult, [ln_, "gB"], [ln_])
                if b["samp"] is not None:
                    tt("dve", lt_[0:r], lt_[0:r], bsl, ALU.add, [ln_, "gB"], [ln_])
                    dma("sp", sguv_d[b["samp"] * 16:(b["samp"] + 1) * 16, pv * 512:(pv + 1) * 512].rearrange("p (c f) -> p c f", c=4),
                        lt_[0:r], reads=[ln_], out_dma=True)
                    cp("dve", vsl, lt_[0:r], [ln_], names)
                else:
                    tt("dve", vsl, lt_[0:r], bsl, ALU.add, [ln_, "gB"], names)
        for pu in range(6):
            wt, wn = wslot()
            wv = wt[:, 0:4096].rearrange("p (k n) -> p k n", k=8)
            wns = wdma(wv, src[:, :, pu * 512:(pu + 1) * 512], wn, 8)
            for cc in range(4):
                c = pu * 4 + cc
                g = c // 6
                for (c0, n, bl) in st["tiles"]:
                    ps, pn = psum()
                    for kc in range(8):
                        mm(ps[:, 0:n], wv[:, kc, cc * 128:(cc + 1) * 128], hT[:, kc, c0:c0 + n], kc == 0, kc == 7, wns + [hn(st, kc, c0)], [pn])
                    bs_ = it % 2
                    it += 1
                    act(ut[bs_][:, 0:n], ps[:, 0:n], AF.Gelu_apprx_tanh, [pn], [("vt", bs_)])
                    pm, pmn = psum()
                    for bi in bl:
                        b = blocks[bi]
                        r = b["rows"]
                        o0 = b["c0"] - c0
                        wmix = wsb16[0:r, g, 0:r] if b["samp"] is not None else wsb[0:r, g, 0:r]
                        mm(pm[:, o0:o0 + r], vg[0:r, c, bi * 128:(bi + 1) * 128], wmix, True, False,
                           [("vv", c), "wsb", "wsb16"], [pmn])
                        mm(pm[:, o0:o0 + r], ones[0:1, :], bsb[0:1, g, 0:r], False, True, ["ones", "bsb"], [pmn])
                    tt("dve", vg[:, c, c0:c0 + n], pm[:, 0:n], ut[bs_][:, 0:n], ALU.mult, [pmn, ("vt", bs_)], [("vgt", c)])
        for pc in range(4):
            wt, wn = wslot()
            wv = wt[:, 0:6144].rearrange("p (k n) -> p k n", k=24)
            wns = wdma(wv, sgu_w_out_d[:, pc * 256:(pc + 1) * 256].rearrange("(k p) n -> p k n", p=128), wn, 24)
            for o2 in range(2):
                oc = pc * 2 + o2
                for (c0, n, _) in st["tiles"]:
                    ps, pn = psum()
                    for kc in range(24):
                        mm(ps[:, 0:n], wv[:, kc, o2 * 128:(o2 + 1) * 128], vg[:, kc, c0:c0 + n], kc == 0, kc == 23,
                           wns + [("vgt", kc)], [pn])
                    resid_evac(st, ps, pn, oc, c0, n, 1, 16, [])
        A.reset(m0)

    step = [0]

    def go():
        step[0] += 1
        return step[0] <= STOP

    for si, st in enumerate(STS):
        if not go():
            break
        load_x(st)
        norm_mod(st, 0, 0)
        if not go():
            break
        if st["mode"] == "pre":
            retention(st, cspre_d, st["c0"])
            P.barrier()
            continue
        retention(st, cs_d, st["c0"])
        P.barrier()
        last = si == len(STS) - 1
        if last:
            for h in range(HEADS):
                dma("sp", retp_d[h].rearrange("(c p) e -> p c e", p=128), Sst[:, h], reads=[("S", 0), ("S", 1)], out_dma=True)
        if not go():
            break
        norm_mod(st, 0, 1)
        conv_ffn(st, 0, last)
        P.barrier()
        if not go():
            break
        norm_mod(st, 1, 0)
        sgu(st)
        P.barrier()
        if not go():
            break
        norm_mod(st, 1, 1)
        conv_ffn(st, 1, last)
        P.barrier()
        if not go():
            break
        norm_mod(st, 0, 0, final=True)
        P.barrier()
    P.finish("sp")
    P.emit(nc)
    return nc


def _tables(half):
    tabs = np.zeros((128, 1200), np.float32)
    gam = np.array([1.0 - 2.0 ** (-5.0 - h) for h in range(HEADS)], np.float64)
    s = np.arange(128)[:, None]
    l = np.arange(128)[None, :]
    for h in range(HEADS):
        m = gam[h] ** np.abs(l - s) * ((s // 64) <= (l // 64)) / 16.0
        tabs[:, h * 128:(h + 1) * 128] = m
        s16 = np.arange(16)[:, None]
        l16 = np.arange(16)[None, :]
        tabs[0:16, 512 + h * 16:512 + (h + 1) * 16] = gam[h] ** np.abs(l16 - s16) / 16.0
        tabs[:, 576 + h * 128:576 + (h + 1) * 128] = (gam[h] ** (np.arange(128) + 1.0) / 16.0)[None, :]
        tabs[:, 1088 + h * 16:1088 + (h + 1) * 16] = (gam[h] ** (np.arange(16) + 1.0) / 16.0)[None, :]
        tabs[:, 1152 + h] = gam[h] ** (127.0 - np.arange(128))
        tabs[:, 1156 + h] = tabs[:, 1152 + h] * float(half)
        tabs[0:16, 1160 + h] = gam[h] ** (15.0 - np.arange(16))
    tabs[:, 1196] = float(half)
    return tabs


def _cossin(pos):
    inv = np.power(np.float32(10000.0), -np.arange(128, dtype=np.float32) / np.float32(128))
    ang = inv[:, None].astype(np.float32) * pos[None, :].astype(np.float32)
    return np.stack([np.cos(ang), np.sin(ang)], axis=1).astype(np.float32)


_NC_CACHE = {}


def kernel(x_prompt, x_sample, state_ret, state_ffn_conv, c_prompt, c_sample,
           w_ada, b_ada, norm_mix_g, norm_ffn_g, ret_w_in, ret_gn_g, ret_w_out,
           sgu_w_in, sgu_ln_g, sgu_ln_b, sgu_w_s, sgu_b_s, sgu_w_out,
           ffn_w_up, ffn_conv_w, ffn_conv_b, ffn_w_down, final_g):
    f32 = lambda a: np.ascontiguousarray(np.asarray(a, dtype=np.float32))
    x_prompt, x_sample, state_ret, state_ffn_conv = map(f32, (x_prompt, x_sample, state_ret, state_ffn_conv))
    c_prompt, c_sample = f32(c_prompt), f32(c_sample)

    def fm(v, nch):
        return np.asarray(v, np.float32).reshape(nch, 128).T

    vecs = np.zeros((128, 512), np.float32)
    for l in range(2):
        vecs[:, l * 8:(l + 1) * 8] = fm(norm_mix_g[l], 8)
        vecs[:, 16 + l * 8:16 + (l + 1) * 8] = fm(norm_ffn_g[l], 8)
        vecs[:, 56 + l * 48:56 + (l + 1) * 48] = fm(b_ada[l], 48)
        cwl = np.asarray(ffn_conv_w[l], np.float32)
        vecs[:, 152 + l * 132:152 + (l + 1) * 132] = cwl.T.reshape(44, 128, 3).transpose(1, 0, 2).reshape(128, 132)
        vecs[:, 416 + l * 44:416 + (l + 1) * 44] = fm(ffn_conv_b[l], 44)
    vecs[:, 32:40] = fm(final_g, 8)
    vecs[:, 40:56] = fm(ret_gn_g[0], 16)
    gb = np.ascontiguousarray(np.broadcast_to(
        np.stack([np.asarray(sgu_ln_g[0], np.float32), np.asarray(sgu_ln_b[0], np.float32)])[None], (128, 2, SGUD)))
    wsT = np.ascontiguousarray(np.asarray(sgu_w_s[0], np.float32).transpose(2, 0, 1))
    bs = np.ascontiguousarray(np.asarray(sgu_b_s[0], np.float32)[None])
    shared = {
        "vecs": vecs, "gb": gb, "sgu_wsT": wsT, "sgu_bs": bs, "identd": np.eye(128, dtype=np.float32),
        "w_ada": f32(w_ada), "ret_w_in": f32(ret_w_in[0]), "ret_w_out": f32(ret_w_out[0]),
        "sgu_w_in": f32(sgu_w_in[0]), "sgu_w_out": f32(sgu_w_out[0]),
        "ffn_w_up": f32(ffn_w_up), "ffn_w_down": f32(ffn_w_down),
    }
    in_maps = []
    for c in range(8):
        seq, half = c // 2, c % 2
        xT = np.zeros((D, NCOL), np.float32)
        xpre = np.zeros((D, NPRE), np.float32)
        if half:
            xT[:, 0:256] = x_prompt[seq, 1792:2048].T
            xpre[:, :] = x_prompt[seq, 0:1792].T
        xT[:, 256:NR] = x_prompt[seq, half * 2048:(half + 1) * 2048].T
        xT[:, NR:] = x_sample[4 * c:4 * c + 4].reshape(64, D).T
        cT = np.concatenate([c_prompt[seq:seq + 1], c_sample[4 * c:4 * c + 4]], 0).T
        pos = np.concatenate([np.arange(NR, dtype=np.float32) + (1792.0 if half else -256.0),
                              np.tile(np.arange(16, dtype=np.float32) + 1024.0, 4)])
        m = dict(shared)
        m.update({
            "xT": xT, "xpre": xpre, "cT": np.ascontiguousarray(cT.reshape(8, 128, 5).transpose(1, 0, 2).reshape(128, 40)),
            "sret": np.ascontiguousarray(state_ret[0, 4 * c:4 * c + 4]),
            "sconvT": np.ascontiguousarray(state_ffn_conv[:, 4 * c:4 * c + 4].transpose(0, 3, 1, 2).reshape(2, 44, 128, 4, 2).transpose(0, 2, 1, 3, 4).reshape(2, 128, 352)),
            "cs": _cossin(pos), "cspre": _cossin(np.arange(NPRE, dtype=np.float32)),
            "tabs": _tables(half),
        })
        in_maps.append(m)
    if "nc" not in _NC_CACHE:
        _NC_CACHE["nc"] = build_program()
    res = run_bass_kernel_spmd(_NC_CACHE["nc"], in_maps, core_ids=list(range(8)))
    R = res.results
    y_prompt = np.zeros((4, 4096, D), np.float32)
    y_sample = np.zeros((32, 16, D), np.float32)
    ret_p = np.zeros((1, 4, HEADS, DK, DV), np.float32)
    ret_s = np.zeros((1, 32, HEADS, DK, DV), np.float32)
    conv_p = np.zeros((2, 4, 2, 2 * FFN), np.float32)
    conv_s = np.zeros((2, 32, 2, 2 * FFN), np.float32)
    sgu_v = np.zeros((1, 32, 16, SGUD), np.float32)
    for c in range(8):
        seq, half = c // 2, c % 2
        r = R[c]
        yT = r["yT"]
        y_prompt[seq, half * 2048:(half + 1) * 2048] = yT[:, 256:NR].T
        y_sample[4 * c:4 * c + 4] = yT[:, NR:].T.reshape(4, 16, D)
        ret_s[0, 4 * c:4 * c + 4] = r["rets"]
        conv_s[:, 4 * c:4 * c + 4] = r["convs"].reshape(2, 128, 44, 4, 2).transpose(0, 3, 4, 2, 1).reshape(2, 4, 2, 2 * FFN)
        sgu_v[0, 4 * c:4 * c + 4] = r["sguv"].reshape(4, 16, SGUD)
        if half:
            ret_p[0, seq] = r["retp"]
            conv_p[:, seq] = r["convp"].reshape(2, 128, 44, 2).transpose(0, 3, 2, 1).reshape(2, 2, 2 * FFN)
    return (y_prompt, y_sample, ret_p, ret_s, conv_p, conv_s, sgu_v)
```

```python
from contextlib import ExitStack
import numpy as np
import ml_dtypes
import concourse.bass as bass
import concourse.mybir as mybir
from concourse.bass_utils import run_bass_kernel_spmd

F32 = mybir.dt.float32
BF16 = mybir.dt.bfloat16
ALU = mybir.AluOpType
AF = mybir.ActivationFunctionType

ENGS = ("pe", "act", "dve", "pool", "sp")
SEM_LIMIT = 30000
DMA_POOL = 48
SBUF_LIMIT = 229344
SBUF_BASE = 16512

D = 1024
HEADS = 4
DK = 256
DV = 512
FFN = 2816
SGUD = 3072
EPS = 1e-6
NPRE = 1792
NR = 2304
NS = 64
NCOL = NR + NS
WMAX = 832
NSLOT = 10
STOP = 10 ** 9
DBG = set()


class Op:
    __slots__ = ("eng", "fn", "deps", "sig", "token", "is_dma", "out_dma")

    def __init__(self, eng, fn, is_dma):
        self.eng = eng
        self.fn = fn
        self.deps = []
        self.sig = False
        self.token = None
        self.is_dma = is_dma
        self.out_dma = False


class Prog:
    def __init__(self):
        self.ops = {e: [] for e in ENGS}
        self.all = []
        self.lastw = {}
        self.readers = {}
        self.dmas = []
        self.dmas_q = {"hw": [], "sw": []}
        self.bar = []
        self.bar_seen = set(ENGS)
        self.dma_since = []

    def barrier(self):
        lasts = [self.ops[e][-1] for e in ENGS if self.ops[e] and e != "pool"]
        pc = [o for o in self.ops["pool"] if not o.is_dma]
        if pc:
            lasts.append(pc[-1])
        self.bar = lasts + list(self.dma_since)
        self.dma_since = []
        self.bar_seen = {"pool"}

    def add(self, eng, fn, reads=(), writes=(), dma=False, out_dma=False):
        op = Op(eng, fn, dma)
        op.out_dma = out_dma
        deps = []
        if eng not in self.bar_seen:
            deps.extend(self.bar)
            self.bar_seen.add(eng)
        for r in reads:
            w = self.lastw.get(r)
            if w is not None:
                deps.append(w)
        for r in writes:
            w = self.lastw.get(r)
            if w is not None:
                deps.append(w)
            deps.extend(self.readers.get(r, ()))
        for r in reads:
            self.readers.setdefault(r, []).append(op)
        for r in writes:
            self.lastw[r] = op
            self.readers[r] = []
        if dma:
            lst = self.dmas_q["sw" if eng == "pool" else "hw"]
            j = len(lst)
            if j >= DMA_POOL // 2:
                deps.append(lst[j - DMA_POOL // 2])
            lst.append(op)
            self.dmas.append(op)
            if eng != "pool":
                self.dma_since.append(op)
        seen = set()
        for d in deps:
            if d is op or id(d) in seen:
                continue
            seen.add(id(d))
            if d.eng == "pe" and eng == "pe" and not d.is_dma:
                continue
            op.deps.append(d)
        self.ops[eng].append(op)
        self.all.append(op)
        return op

    def finish(self, eng="sp"):
        op = Op(eng, None, False)
        op.deps = [d for d in self.dmas if d.out_dma]
        self.ops[eng].append(op)
        self.all.append(op)

    def emit(self, nc):
        for op in self.all:
            for d in op.deps:
                d.sig = True
        with ExitStack() as st:
            sems = {}
            for e in ENGS:
                n = sum(1 for o in self.ops[e] if o.sig and not o.is_dma)
                k = max(1, (n + SEM_LIMIT - 1) // SEM_LIMIT)
                sems[e] = [st.enter_context(nc.semaphore(f"pg_{e}{i}")) for i in range(k)]
            hp = DMA_POOL // 2
            dsem = {q: [st.enter_context(nc.semaphore(f"d{q}{i}")) for i in range(hp)] for q in ("hw", "sw")}
            for e in ENGS:
                c = 0
                for o in self.ops[e]:
                    if o.is_dma or not o.sig:
                        continue
                    k = c // SEM_LIMIT
                    o.token = (sems[e][k], c - k * SEM_LIMIT + 1)
                    c += 1
            for q in ("hw", "sw"):
                for j, o in enumerate(self.dmas_q[q]):
                    o.token = (dsem[q][j % hp], 16 * (j // hp + 1))
            block = st.enter_context(nc.Block())

            def run(e):
                def body(eng):
                    seen = {}
                    pos = {id(o_): i_ for i_, o_ in enumerate(self.ops[e])}
                    for o in self.ops[e]:
                        for d in o.deps:
                            if d.eng == e and not d.is_dma and e in ("act", "dve") and pos[id(o)] - pos[id(d)] >= 3:
                                continue
                            s, v = d.token
                            if seen.get(id(s), 0) >= v:
                                continue
                            eng.wait_ge(s, v)
                            seen[id(s)] = v
                        if o.fn is None:
                            continue
                        ins = o.fn(eng)
                        if o.is_dma:
                            ins.then_inc(o.token[0], 16)
                        elif o.sig:
                            ins.then_inc(o.token[0], 1)

                return body

            block.tensor(run("pe"))
            block.scalar(run("act"))
            block.vector(run("dve"))
            block.gpsimd(run("pool"))
            block.sync(run("sp"))


class Alloc:
    def __init__(self, nc):
        self.nc = nc
        self.off = SBUF_BASE
        self.n = 0
        self.peak = 0

    def __call__(self, shape, dtype):
        nb = 1
        for s in shape[1:]:
            nb *= s
        nb *= 4 if dtype == F32 else 2
        off = (self.off + 31) // 32 * 32
        t = self.nc.alloc_sbuf_tensor_at(f"t{self.n}", list(shape), dtype, offset=off)
        self.n += 1
        self.off = off + nb
        self.peak = max(self.peak, self.off)
        assert self.off <= SBUF_LIMIT, f"SBUF overflow {self.off}"
        return t.ap()

    def mark(self):
        return self.off

    def reset(self, m):
        self.off = m


def build_program():
    nc = bass.Bass("TRN2", target_bir_lowering=False)
    P = Prog()
    A = Alloc(nc)

    def din(name, shape):
        return nc.dram_tensor(name, list(shape), F32, kind="ExternalInput").ap()

    def dout(name, shape):
        return nc.dram_tensor(name, list(shape), F32, kind="ExternalOutput").ap()

    xT_d = din("xT", (D, NCOL))
    xpre_d = din("xpre", (D, NPRE))
    cT_d = din("cT", (128, 40))
    sret_d = din("sret", (4, HEADS, DK, DV))
    sconv_d = din("sconvT", (2, 128, 352))
    cs_d = din("cs", (128, 2, NCOL))
    cspre_d = din("cspre", (128, 2, NPRE))
    tabs_d = din("tabs", (128, 1200))
    identd_d = din("identd", (128, 128))
    gb_d = din("gb", (128, 2, SGUD))
    vec_d = din("vecs", (128, 512))
    w_ada_d = din("w_ada", (2, D, 6 * D))
    ret_w_in_d = din("ret_w_in", (D, 6 * D))
    ret_w_out_d = din("ret_w_out", (2 * D, D))
    sgu_w_in_d = din("sgu_w_in", (D, 2 * SGUD))
    sgu_w_out_d = din("sgu_w_out", (SGUD, D))
    sgu_wsT_d = din("sgu_wsT", (128, 4, 128))
    sgu_bs_d = din("sgu_bs", (1, 4, 128))
    ffn_w_up_d = din("ffn_w_up", (2, D, 2 * FFN))
    ffn_w_down_d = din("ffn_w_down", (2, FFN, D))

    yT_d = dout("yT", (D, NCOL))
    retp_d = dout("retp", (HEADS, DK, DV))
    rets_d = dout("rets", (4, HEADS, DK, DV))
    convp_d = dout("convp", (2, 128, 88))
    convs_d = dout("convs", (2, 128, 352))
    sguv_d = dout("sguv", (NS, SGUD))

    psb = [nc.alloc_psum_tensor(f"ps{i}", [128, 512], F32).ap() for i in range(6)]
    pT = nc.alloc_psum_tensor("pT", [128, 1024], BF16).ap()
    pT2 = nc.alloc_psum_tensor("pT2", [128, 1024], BF16).ap()
    ps_ctr = [0]

    def psum():
        i = ps_ctr[0] % 6
        ps_ctr[0] += 1
        return psb[i], ("ps", i)

    def mm(out, lhsT, rhs, start, stop, reads, writes):
        P.add("pe", lambda e: e.matmul(out, lhsT=lhsT, rhs=rhs, start=start, stop=stop), reads, writes)

    def tr(out, in_, ident, reads, writes):
        P.add("pe", lambda e: e.transpose(out=out, in_=in_, identity=ident), reads, writes)

    def act(out, in_, func, reads, writes, scale=1.0, bias=0.0):
        if func == AF.Copy and not (isinstance(scale, float) and isinstance(bias, float)):
            func = AF.Identity
        P.add("act", lambda e: e.activation(out=out, in_=in_, func=func, bias=bias, scale=scale), reads, writes)

    def tt(eng, out, in0, in1, op, reads, writes):
        P.add(eng, lambda e: e.tensor_tensor(out=out, in0=in0, in1=in1, op=op), reads, writes)

    def ts(eng, out, in0, s1, s2, op0, op1, reads, writes):
        if s2 is None:
            P.add(eng, lambda e: e.tensor_scalar(out=out, in0=in0, scalar1=s1, scalar2=None, op0=op0), reads, writes)
        else:
            P.add(eng, lambda e: e.tensor_scalar(out=out, in0=in0, scalar1=s1, scalar2=s2, op0=op0, op1=op1), reads, writes)

    def stt(eng, out, in0, scalar, in1, op0, op1, reads, writes):
        P.add(eng, lambda e: e.scalar_tensor_tensor(out=out, in0=in0, scalar=scalar, in1=in1, op0=op0, op1=op1), reads, writes)

    def cp(eng, out, in_, reads, writes):
        if eng == "act":
            act(out, in_, AF.Copy, reads, writes)
        else:
            P.add(eng, lambda e: e.tensor_copy(out=out, in_=in_), reads, writes)

    def dma(q, out, in_, reads=(), writes=(), out_dma=False):
        P.add(q, lambda e: e.dma_start(out=out, in_=in_), reads, writes, dma=True, out_dma=out_dma)

    def wdma(out3, in3, wn, nk, base=0):
        step = 4
        names = []
        for k0 in range(0, nk, step):
            k1 = min(nk, k0 + step)
            nm_ = (wn, base + k0 // step)
            dma("pool", out3[:, k0:k1, :], in3[:, k0:k1, :], writes=[nm_])
            names.append(nm_)
        return names

    def memset(eng, ap, val, writes):
        P.add(eng, lambda e: e.memset(ap, val), (), writes)

    x = A([128, 8, WMAX], F32)
    hT = A([128, 8, WMAX], BF16)
    rstd = A([128, WMAX], F32)
    Sst = A([128, HEADS, 2, DV], F32)
    tabs = A([128, 1200], F32)
    vecs = A([128, 512], F32)
    mod = A([128, 2, 48, 5], F32)
    mul1 = A([128, 2, 2, 8, 5], F32)
    ident = A([128, 128], BF16)
    ones = A([128, 128], BF16)
    flag = tabs[:, 1196:1197]
    hal = A([128, 2, 44, 2], F32)
    n_sq = A([128, 8, 384], BF16)
    n_xr = [A([128, 384], F32) for _ in range(2)]
    NWS = 3
    wsl = [A([128, 6144], BF16) for _ in range(NWS)]
    ws_ctr = [0]

    def wslot():
        i = ws_ctr[0] % NWS
        ws_ctr[0] += 1
        return wsl[i], ("w", i)

    maskT = tabs[:, 0:512].rearrange("p (h l) -> p h l", h=4)
    mask16 = tabs[:, 512:576].rearrange("p (h l) -> p h l", h=4)
    qd = tabs[:, 576:1088].rearrange("p (h l) -> p h l", h=4)
    qd16 = tabs[:, 1088:1152].rearrange("p (h l) -> p h l", h=4)
    kd_main = tabs[:, 1152:1156]
    kd_halo = tabs[:, 1156:1160]
    kd16 = tabs[:, 1160:1164]
    identf = None
    nmg = vecs[:, 0:16].rearrange("p (l c) -> p l c", l=2)
    nfg = vecs[:, 16:32].rearrange("p (l c) -> p l c", l=2)
    fing = vecs[:, 32:40]
    gng = vecs[:, 40:56]
    bada = vecs[:, 56:152].rearrange("p (l c) -> p l c", l=2)
    cw = vecs[:, 152:416].rearrange("p (l c t) -> p l c t", l=2, t=3)
    cb = vecs[:, 416:504].rearrange("p (l c) -> p l c", l=2)

    GAM = [1.0 - 2.0 ** (-5.0 - h) for h in range(HEADS)]

    dma("sp", tabs, tabs_d, writes=["tabs"])
    dma("sp", vecs, vec_d, writes=["vecs"])
    memset("dve", hal, 0.0, [("hal", l_, c_) for l_ in range(2) for c_ in range(44)])
    memset("dve", Sst, 0.0, [("S", 0), ("S", 1)])
    memset("dve", ones, 1.0, ["ones"])

    cTb = A([128, 8, 5], BF16)
    arena0 = A.mark()

    idf = A([128, 128], F32)
    cTs = A([128, 8, 5], F32)
    dma("sp", idf, identd_d, writes=["idf"])
    cp("dve", ident, idf, ["idf"], ["ident"])
    dma("sp", cTs, cT_d.rearrange("p (c s) -> p c s", c=8), writes=["cTs"])
    act(cTb, cTs, AF.Silu, ["cTs"], ["cTb"])
    def adaln_part(l, pcs):
        for pc in pcs:
            wt, wn = wslot()
            wv = wt[:, 0:4096].rearrange("p (k n) -> p k n", k=8)
            wns = wdma(wv, w_ada_d[l, :, pc * 512:(pc + 1) * 512].rearrange("(k p) n -> p k n", p=128), wn, 8)
            ps, pn = psum()
            for oc in range(4):
                for kc in range(8):
                    mm(ps[:, oc * 8:oc * 8 + 5], wv[:, kc, oc * 128:(oc + 1) * 128], cTb[:, kc, :], kc == 0, kc == 7,
                       wns + ["cTb"], [pn])
            for oc in range(4):
                ch = pc * 4 + oc
                act(mod[:, l, ch, :], ps[:, oc * 8:oc * 8 + 5], AF.Identity, [pn, "vecs"], ["mod"], bias=bada[:, l, ch:ch + 1])

    def mul1_part(l, j):
        gv, sc0 = ((nmg, 8), (nfg, 32))[j]
        for c in range(8):
            ts("dve", mul1[:, l, j, c, :], mod[:, l, sc0 + c, :], 1.0, gv[:, l, c:c + 1], ALU.add, ALU.mult,
               ["mod", "vecs"], ["mul1"])

    adaln_part(0, range(0, 4))
    mul1_part(0, 0)
    ada_todo = [(0, pc) for pc in range(4, 12)] + [(1, pc) for pc in range(12)]

    def ada_step(k):
        for _ in range(k):
            if ada_todo:
                l_, pc_ = ada_todo.pop(0)
                adaln_part(l_, [pc_])
                if not ada_todo:
                    mul1_part(0, 1)
                    mul1_part(1, 0)
                    mul1_part(1, 1)
    P.barrier()
    A.reset(arena0)

    def make_st(mode, src, c0, nblk, halo=0, sample=False):
        blocks = []
        for b in range(nblk):
            blocks.append(dict(c0=b * 128, rows=128, samp=None, kd=(kd_halo if (mode == "pre" or b < halo) else kd_main)))
        W = nblk * 128
        tiles = []
        cc = 0
        while cc < W:
            n = min(384, W - cc)
            tiles.append((cc, n, [i for i in range(nblk) if cc <= i * 128 < cc + n]))
            cc += n
        seqs = [(0, W, 0)]
        if sample:
            for s in range(4):
                blocks.append(dict(c0=W + 16 * s, rows=16, samp=s, kd=kd16))
                seqs.append((W + 16 * s, 16, 1 + s))
            tiles.append((W, 64, [nblk + s for s in range(4)]))
            W += 64
        return dict(mode=mode, src=src, c0=c0, blocks=blocks, tiles=tiles, seqs=seqs, W=W, Wp=nblk * 128,
                    sample=sample, halo=halo)

    STS = [
        make_st("pre", xpre_d, 0, 6), make_st("pre", xpre_d, 768, 6), make_st("pre", xpre_d, 1536, 2),
        make_st("full", xT_d, 0, 6, halo=2), make_st("full", xT_d, 768, 6), make_st("full", xT_d, 1536, 6, sample=True),
    ]

    def norm_mod(st, l, j, final=False):
        W = st["W"]
        m0 = A.mark()
        sqs = [n_sq]
        xrs = n_xr
        xi = [0]
        sd = rstd
        for ti_, (c0, n, _) in enumerate(st["tiles"]):
            sq = sqs[0]
            sqn = ("sq", 0)
            for kc in range(8):
                act(sq[:, kc, 0:n], x[:, kc, c0:c0 + n], AF.Square, [("x", kc)], [sqn])
            ps, pn = psum()
            for kc in range(8):
                mm(ps[:, 0:n], ones, sq[:, kc, 0:n], kc == 0, kc == 7, ["ones", sqn], [pn])
            act(sd[:, c0:c0 + n], ps[:, 0:n], AF.Copy, [pn], ["rstd"])
        ts("dve", sd[:, 0:W], sd[:, 0:W], 1.0 / D, EPS, ALU.mult, ALU.add, ["rstd"], ["rstd"])
        act(sd[:, 0:W], sd[:, 0:W], AF.Sqrt, ["rstd"], ["rstd"])
        P.add("dve", lambda e: e.reciprocal(out=rstd[:, 0:W], in_=sd[:, 0:W]), ["rstd"], ["rstd"])
        if final:
            yo = A([128, 8, WMAX], F32)
            for kc in range(8):
                stt("dve", yo[:, kc, 0:W], x[:, kc, 0:W], fing[:, kc:kc + 1], rstd[:, 0:W], ALU.mult, ALU.mult,
                    [("x", kc), "rstd", "vecs"], [("yo", kc)])
                dma("sp", yT_d[kc * 128:(kc + 1) * 128, st["c0"]:st["c0"] + W], yo[:, kc, 0:W], reads=[("yo", kc)], out_dma=True)
        else:
            sh0 = 0 if j == 0 else 24
            for kc in range(8):
                for (c0, n, sq_) in st["seqs"]:
                    for cc in range(c0, c0 + n, 384):
                        nn = min(384, c0 + n - cc)
                        xr = xrs[xi[0] % 2]
                        xrn = ("xr", xi[0] % 2)
                        xi[0] += 1
                        stt("dve", xr[:, 0:nn], x[:, kc, cc:cc + nn], mul1[:, l, j, kc, sq_:sq_ + 1], rstd[:, cc:cc + nn],
                            ALU.mult, ALU.mult, [("x", kc), "rstd", "mul1"], [xrn])
                        act(hT[:, kc, cc:cc + nn], xr[:, 0:nn], AF.Identity, [xrn, "mod"], [hn(st, kc, cc)],
                            bias=mod[:, l, sh0 + kc, sq_:sq_ + 1])
        A.reset(m0)

    def hn(st, kc, col):
        return ("hT", kc, col // 384 if col < st["Wp"] else 9)

    def resid_evac(st, ps, pn, oc, c0, n, l, gch, reads):
        for (s0, sn, sq_) in st["seqs"]:
            a = max(s0, c0)
            b = min(s0 + sn, c0 + n)
            if a >= b:
                continue
            stt("dve", x[:, oc, a:b], ps[:, a - c0:b - c0], mod[:, l, gch + oc, sq_:sq_ + 1], x[:, oc, a:b],
                ALU.mult, ALU.add, [pn, "mod", ("x", oc)] + reads, [("x", oc)])

    def load_x(st):
        W = st["W"]
        for kc in range(8):
            dma("sp", x[:, kc, 0:W], st["src"][kc * 128:(kc + 1) * 128, st["c0"]:st["c0"] + W], writes=[("x", kc)])

    def retention(st, cs_all, cs_c0):
        pre = st["mode"] == "pre"
        W = st["W"]
        blocks = st["blocks"]
        m0 = A.mark()
        cs = A([128, 2, WMAX], F32)
        dma("sp", cs[:, :, 0:W], cs_all[:, :, cs_c0:cs_c0 + W], writes=["cs"])
        kT = A([128, 2, WMAX], BF16)
        vtok = A([128, NSLOT, DV], BF16)
        kdt = A([128, NSLOT, DK], BF16)
        rts = [[A([128, 384], F32) for _ in range(4)] for _ in range(2)]
        rti = [0]
        Sbf = A([128, 2, DV], BF16)
        if not pre:
            qT = A([128, 2, WMAX], BF16)
            qdT = A([128, 2, WMAX], BF16)
            sgT = A([128, 4, WMAX], BF16)
            yT = A([128, 16, WMAX], BF16)
            psb16_l = [A([128, 128], BF16) for _ in range(2)]
            onorm_l = [A([128, DV], BF16) for _ in range(2)]
            bst_l = [A([128, 6], F32) for _ in range(2)]
            bmv_l = [A([128, 2], F32) for _ in range(2)]
            brs_l = [A([128, 1], F32) for _ in range(2)]
            bnb_l = [A([128, 1], F32) for _ in range(2)]
            if st["sample"]:
                S0f = A([128, 2, DV], F32)
                S0b = A([128, 2, DV], BF16)
                Sof = A([128, 2, DV], F32)

        def rotary(p1, p2, pn1, pn2, dst, c0, n, rname):
            cosv = cs[:, 0, c0:c0 + n]
            sinv = cs[:, 1, c0:c0 + n]
            k_ = rti[0] % 2
            rti[0] += 1
            rt = rts[k_]
            tt("dve", rt[0][:, 0:n], p1[:, 0:n], cosv, ALU.mult, [pn1, "cs"], [("rt0", k_)])
            tt("dve", rt[1][:, 0:n], p2[:, 0:n], sinv, ALU.mult, [pn2, "cs"], [("rt1", k_)])
            tt("dve", rt[2][:, 0:n], p1[:, 0:n], sinv, ALU.mult, [pn1, "cs"], [("rt2", k_)])
            tt("dve", rt[3][:, 0:n], p2[:, 0:n], cosv, ALU.mult, [pn2, "cs"], [("rt3", k_)])
            tt("dve", dst[:, 0, c0:c0 + n], rt[0][:, 0:n], rt[1][:, 0:n], ALU.subtract, [("rt0", k_), ("rt1", k_)], [rname])
            tt("dve", dst[:, 1, c0:c0 + n], rt[2][:, 0:n], rt[3][:, 0:n], ALU.add, [("rt2", k_), ("rt3", k_)], [rname])

        for h in range(HEADS):
            wqk, nqk = wslot()
            wqkv = wqk[:, 0:4096].rearrange("p (k n) -> p k n", k=8)
            src = ret_w_in_d.rearrange("(k p) n -> p k n", p=128)
            nqk_l = wdma(wqkv[:, :, 0:256], src[:, :, h * DK:(h + 1) * DK], nqk, 8, 0) + \
                wdma(wqkv[:, :, 256:512], src[:, :, D + h * DK:D + (h + 1) * DK], nqk, 8, 1)
            wv_, nv = wslot()
            wvv = wv_[:, 0:4096].rearrange("p (k n) -> p k n", k=8)
            nv_l = wdma(wvv, src[:, :, 2 * D + h * DV:2 * D + (h + 1) * DV], nv, 8)
            if not pre:
                wg_, ng = wslot()
                wgv = wg_[:, 0:4096].rearrange("p (k n) -> p k n", k=8)
                ng_l = wdma(wgv, src[:, :, 4 * D + h * DV:4 * D + (h + 1) * DV], ng, 8)
            for (c0, n, bl) in st["tiles"]:
                for which in ((1,) if pre else (0, 1)):
                    pp = []
                    for dc in range(2):
                        ps, pn = psum()
                        for kc in range(8):
                            mm(ps[:, 0:n], wqkv[:, kc, which * 256 + dc * 128:which * 256 + (dc + 1) * 128],
                               hT[:, kc, c0:c0 + n], kc == 0, kc == 7, nqk_l + [hn(st, kc, c0)], [pn])
                        pp.append((ps, pn))
                    rotary(pp[0][0], pp[1][0], pp[0][1], pp[1][1], kT if which else qT, c0, n, "kT" if which else "qT")
                for bi in bl:
                    b = blocks[bi]
                    r = b["rows"]
                    ps, pn = psum()
                    for kc in range(8):
                        mm(ps[0:r, :], hT[:, kc, b["c0"]:b["c0"] + r], wvv[:, kc, :], kc == 0, kc == 7, nv_l + [hn(st, kc, b["c0"])], [pn])
                    cp("act", vtok[0:r, bi, :], ps[0:r, :], [pn], [("vtok", bi)])
                if not pre:
                    for ec in range(4):
                        ps, pn = psum()
                        for kc in range(8):
                            mm(ps[:, 0:n], wgv[:, kc, ec * 128:(ec + 1) * 128], hT[:, kc, c0:c0 + n], kc == 0, kc == 7,
                               ng_l + [hn(st, kc, c0)], [pn])
                        act(sgT[:, ec, c0:c0 + n], ps[:, 0:n], AF.Silu, [pn], ["sgT"])
                        ts("dve", sgT[:, ec, c0:c0 + n], sgT[:, ec, c0:c0 + n], gng[:, h * 4 + ec:h * 4 + ec + 1], None, ALU.mult, None,
                           ["sgT", "vecs"], ["sgT"])
            pend_tail = []

            def emit_tail(on_, onn_, r_, c0_, h=h):
                for ec in range(4):
                    tr(pT2[:, ec * 128:ec * 128 + r_], on_[0:r_, ec * 128:(ec + 1) * 128], ident[0:r_, 0:r_],
                       [onn_, "ident"], ["pT2"])
                tt("dve", yT[:, h * 4:(h + 1) * 4, c0_:c0_ + r_], pT2[:, 0:512].rearrange("p (e l) -> p e l", e=4)[:, :, 0:r_],
                   sgT[:, :, c0_:c0_ + r_], ALU.mult, ["pT2", "sgT"], [("yT", h)])

            if not pre:
                for dc_ in range(2):
                    cp("act", Sbf[:, dc_, :], Sst[:, h, dc_, :], [("S", dc_)], [("Sbf", dc_)])
            p_idx = [i for i, b_ in enumerate(blocks) if b_["samp"] is None]
            s_idx = [i for i, b_ in enumerate(blocks) if b_["samp"] is not None]
            order = []
            while p_idx or s_idx:
                if p_idx:
                    order.append(p_idx.pop(0))
                if s_idx:
                    order.append(s_idx.pop(0))
            for oi, bi in enumerate(order):
                b = blocks[bi]
                r = b["rows"]
                c0 = b["c0"]
                samp = b["samp"]
                L = 16 if samp is not None else 128
                if not pre:
                    rb = oi % 2
                    psb16, onorm, bst, bmv, brs, bnb = psb16_l[rb], onorm_l[rb], bst_l[rb], bmv_l[rb], brs_l[rb], bnb_l[rb]
                    nm = lambda base: (base, rb)
                if samp is not None and "noR" in DBG:
                    continue
                for dc in range(2):
                    tr(pT[0:r, dc * 128:(dc + 1) * 128], kT[:, dc, c0:c0 + r], ident, ["kT", "ident"], ["pT"])
                for dc in range(2):
                    act(kdt[0:r, bi, dc * 128:(dc + 1) * 128], pT[0:r, dc * 128:(dc + 1) * 128], AF.Copy, ["pT", "tabs"],
                        [("kdt", bi)], scale=b["kd"][0:r, h:h + 1])
                if samp is not None and "noD" not in DBG:
                    dma("sp", S0f, sret_d[samp, h].rearrange("(c p) e -> p c e", p=128), writes=["S0f"])
                    cp("act", S0b, S0f, ["S0f"], ["S0b"])
                    Sb_, Sbn = S0b, "S0b"
                elif samp is not None:
                    Sb_, Sbn = S0b, "S0b"
                else:
                    Sb_, Sbn = Sbf, "Sbf"
                do_o = not pre and not (samp is not None and "noO" in DBG)
                if do_o:
                    ps, pn = psum()
                    for dc in range(2):
                        mm(ps[0:r, 0:r], kT[:, dc, c0:c0 + r], qT[:, dc, c0:c0 + r], dc == 0, dc == 1, ["kT", "qT"], [pn])
                    mk = mask16[0:r, h, 0:r] if samp is not None else maskT[0:r, h, 0:r]
                    tt("dve", psb16[0:r, 0:r], ps[0:r, 0:r], mk, ALU.mult, [pn, "tabs"], [nm("psb16")])
                    qdv = (qd16 if samp is not None else qd)[:, h, 0:r]
                    for dc in range(2):
                        tt("dve", qdT[:, dc, c0:c0 + r], qT[:, dc, c0:c0 + r], qdv, ALU.mult, ["qT", "tabs"], [("qdT", bi)])
                    po, pon = psum()
                    mm(po[0:r, :], psb16[0:r, 0:r], vtok[0:r, bi, :], True, False, [nm("psb16"), ("vtok", bi)], [pon])
                    for dc in range(2):
                        mm(po[0:r, :], qdT[:, dc, c0:c0 + r], Sb_[:, dc, :], False, dc == 1,
                           [("qdT", bi), (Sbn, dc) if Sbn == "Sbf" else Sbn], [pon])
                if samp is not None and "noS" in DBG:
                    continue
                dec = float(np.float32(GAM[h]) ** np.float32(L))
                for dc in range(2):
                    ps, pn = psum()
                    mm(ps[:, :], kdt[0:r, bi, dc * 128:(dc + 1) * 128], vtok[0:r, bi, :], True, True,
                       [("kdt", bi), ("vtok", bi)], [pn])
                    if samp is not None:
                        stt("dve", Sof[:, dc, :], S0f[:, dc, :], dec, ps[:, :], ALU.mult, ALU.add, ["S0f", pn], ["Sof"])
                    else:
                        stt("dve", Sst[:, h, dc, :], Sst[:, h, dc, :], dec, ps[:, :], ALU.mult, ALU.add, [("S", dc), pn], [("S", dc)])
                        if not pre:
                            cp("act", Sbf[:, dc, :], Sst[:, h, dc, :], [("S", dc)], [("Sbf", dc)])
                if samp is not None and "noD" in DBG:
                    pass
                elif samp is not None:
                    dma("sp", rets_d[samp, h].rearrange("(c p) e -> p c e", p=128), Sof, reads=["Sof"], out_dma=True)
                if do_o:
                    P.add("dve", (lambda o_, i_: (lambda e: e.bn_stats(out=o_, in_=i_)))(bst[0:r, :], po[0:r, :]), [pon], [nm("bst")])
                    P.add("dve", (lambda o_, i_: (lambda e: e.bn_aggr(out=o_, in_=i_)))(bmv[0:r, :], bst[0:r, :]), [nm("bst")], [nm("bmv")])
                    ts("dve", brs[0:r, :], bmv[0:r, 1:2], EPS, None, ALU.add, None, [nm("bmv")], [nm("brs")])
                    act(brs[0:r, :], brs[0:r, :], AF.Sqrt, [nm("brs")], [nm("brs")])
                    P.add("dve", (lambda o_, i_: (lambda e: e.reciprocal(out=o_, in_=i_)))(brs[0:r, :], brs[0:r, :]), [nm("brs")], [nm("brs")])
                    stt("dve", bnb[0:r, :], bmv[0:r, 0:1], -1.0, brs[0:r, 0:1], ALU.mult, ALU.mult, [nm("bmv"), nm("brs")], [nm("bnb")])
                    act(onorm[0:r, :], po[0:r, :], AF.Identity, [pon, nm("brs"), nm("bnb")], [nm("onorm")],
                        scale=brs[0:r, 0:1], bias=bnb[0:r, 0:1])
                    pend_tail.append((onorm, nm("onorm"), r, c0))
                    if len(pend_tail) > 1:
                        emit_tail(*pend_tail.pop(0))
            while pend_tail:
                emit_tail(*pend_tail.pop(0))
            if pre:
                ada_step(2)
        if not pre:
            ada_step(100)
            for pc in range(4):
                wo, no = wslot()
                wov = wo[:, 0:4096].rearrange("p (k n) -> p k n", k=16)
                no_l = wdma(wov, ret_w_out_d[:, pc * 256:(pc + 1) * 256].rearrange("(k p) n -> p k n", p=128), no, 16)
                for o2 in range(2):
                    oc = pc * 2 + o2
                    for (c0, n, _) in st["tiles"]:
                        ps, pn = psum()
                        for kc in range(16):
                            mm(ps[:, 0:n], wov[:, kc, o2 * 128:(o2 + 1) * 128], yT[:, kc, c0:c0 + n], kc == 0, kc == 15,
                               no_l + [("yT", kc // 4)], [pn])
                        resid_evac(st, ps, pn, oc, c0, n, 0, 16, [])
        A.reset(m0)

    def conv_ffn(st, l, last):
        W = st["W"]
        m0 = A.mark()
        mT = A([128, 22, WMAX], BF16)
        ab = [[A([128, 386], F32) for _ in range(2)] for _ in range(2)]
        acc = [[A([128, 384], F32) for _ in range(2)] for _ in range(2)]
        sgb = [A([128, 384], F32) for _ in range(2)]
        if st["sample"]:
            scs = A([128, 44, 4, 2], F32)
            dma("sp", scs, sconv_d[l].rearrange("p (c s r) -> p c s r", c=44, s=4), writes=["scs"])
            abs_ = [A([128, 4, 18], F32) for _ in range(2)]
            accs = [A([128, 4, 16], F32) for _ in range(2)]
            sgs = A([128, 4, 16], F32)
            csts = A([128, 44, 4, 2], F32)
        if last:
            cstp = A([128, 44, 2], F32)
        if st["halo"]:
            ts("dve", hT[:, :, 254:256], hT[:, :, 254:256], flag, None, ALU.mult, None, [("hT", k, 0) for k in range(8)] + ["tabs"],
               [("hT", k, 0) for k in range(8)])
        it = 0
        for pj in range(11):
            wt, wn = wslot()
            wv = wt[:, 0:4096].rearrange("p (k n) -> p k n", k=8)
            src = ffn_w_up_d[l].rearrange("(k p) n -> p k n", p=128)
            wns = wdma(wv[:, :, 0:256], src[:, :, pj * 256:(pj + 1) * 256], wn, 8, 0) + \
                wdma(wv[:, :, 256:512], src[:, :, FFN + pj * 256:FFN + (pj + 1) * 256], wn, 8, 1)
            for jj in range(2):
                j = pj * 2 + jj
                chs = (j, 22 + j)
                for ti, (c0, n, bl) in enumerate(st["tiles"]):
                    is_s = st["sample"] and ti == len(st["tiles"]) - 1
                    pps = []
                    for gv in range(2):
                        ps, pn = psum()
                        for kc in range(8):
                            mm(ps[:, 0:n], wv[:, kc, gv * 256 + jj * 128:gv * 256 + (jj + 1) * 128], hT[:, kc, c0:c0 + n],
                               kc == 0, kc == 7, wns + [hn(st, kc, c0)], [pn])
                        pps.append((ps, pn))
                    bsel = it % 2
                    it += 1
                    nprompt = len(st["tiles"]) - (1 if st["sample"] else 0)
                    for gv in range(2):
                        ps, pn = pps[gv]
                        ch = chs[gv]
                        if not is_s:
                            a_ = ab[bsel][gv]
                            an = ("ab", bsel, gv)
                            ahn = ("abh", bsel, gv)
                            ac_ = acc[bsel][gv]
                            acn = ("acc", bsel, gv)
                            if ti == 0:
                                cp("act", a_[:, 0:2], hal[:, l, ch, :], [("hal", l, ch)], [ahn])
                            act(a_[:, 2:2 + n], ps[:, 0:n], AF.Copy, [pn], [an])
                            if ti + 1 < nprompt:
                                cp("act", ab[1 - bsel][gv][:, 0:2], ps[:, n - 2:n], [pn], [("abh", 1 - bsel, gv)])
                            else:
                                cp("act", hal[:, l, ch, :], ps[:, n - 2:n], [pn], [("hal", l, ch)])
                                if last:
                                    cp("act", cstp[:, ch, :], ps[:, n - 2:n], [pn], ["cstp"])
                            act(ac_[:, 0:n], ps[:, 0:n], AF.Identity, [pn, "vecs"], [acn], scale=cw[:, l, ch, 2:3], bias=cb[:, l, ch:ch + 1])
                            stt("dve", ac_[:, 0:n], a_[:, 1:1 + n], cw[:, l, ch, 1:2], ac_[:, 0:n], ALU.mult, ALU.add, [an, ahn, acn, "vecs"], [acn])
                            stt("dve", ac_[:, 0:n], a_[:, 0:n], cw[:, l, ch, 0:1], ac_[:, 0:n], ALU.mult, ALU.add, [an, ahn, acn, "vecs"], [acn])
                        else:
                            a_ = abs_[gv]
                            an = ("abs", gv)
                            ac_ = accs[gv]
                            acn = ("accs", gv)
                            cp("act", a_[:, :, 0:2], scs[:, ch, :, :], ["scs"], [an])
                            act(a_[:, :, 2:18], ps[:, 0:64].rearrange("p (s t) -> p s t", s=4), AF.Copy, [pn], [an])
                            act(ac_, ps[:, 0:64].rearrange("p (s t) -> p s t", s=4), AF.Identity, [pn, "vecs"], [acn],
                                scale=cw[:, l, ch, 2:3], bias=cb[:, l, ch:ch + 1])
                            stt("dve", ac_, a_[:, :, 1:17], cw[:, l, ch, 1:2], ac_, ALU.mult, ALU.add, [an, acn, "vecs"], [acn])
                            stt("dve", ac_, a_[:, :, 0:16], cw[:, l, ch, 0:1], ac_, ALU.mult, ALU.add, [an, acn, "vecs"], [acn])
                            cp("act", csts[:, ch, :, :], a_[:, :, 16:18], [an], ["csts"])
                    if not is_s:
                        act(sgb[bsel][:, 0:n], acc[bsel][0][:, 0:n], AF.Silu, [("acc", bsel, 0)], [("sgb", bsel)])
                        tt("dve", mT[:, j, c0:c0 + n], sgb[bsel][:, 0:n], acc[bsel][1][:, 0:n], ALU.mult,
                           [("sgb", bsel), ("acc", bsel, 1)], [("mT", j)])
                    else:
                        act(sgs, accs[0], AF.Silu, [("accs", 0)], ["sgs"])
                        tt("dve", mT[:, j, c0:c0 + 64].rearrange("p (s t) -> p s t", s=4), sgs, accs[1], ALU.mult,
                           ["sgs", ("accs", 1)], [("mT", j)])
        if st["sample"]:
            dma("sp", convs_d[l].rearrange("p (c s r) -> p c s r", c=44, s=4), csts, reads=["csts"], out_dma=True)
        if last:
            dma("sp", convp_d[l].rearrange("p (c r) -> p c r", c=44), cstp, reads=["cstp"], out_dma=True)
        for pc in range(4):
            wt, wn = wslot()
            wv = wt[:, 0:5632].rearrange("p (k n) -> p k n", k=22)
            wns = wdma(wv, ffn_w_down_d[l][:, pc * 256:(pc + 1) * 256].rearrange("(k p) n -> p k n", p=128), wn, 22)
            for o2 in range(2):
                oc = pc * 2 + o2
                for (c0, n, _) in st["tiles"]:
                    ps, pn = psum()
                    for kc in range(22):
                        mm(ps[:, 0:n], wv[:, kc, o2 * 128:(o2 + 1) * 128], mT[:, kc, c0:c0 + n], kc == 0, kc == 21,
                           wns + [("mT", kc)], [pn])
                    resid_evac(st, ps, pn, oc, c0, n, l, 40, [])
        A.reset(m0)

    def sgu(st):
        W = st["W"]
        blocks = st["blocks"]
        nb = len(blocks)
        m0 = A.mark()
        vg = A([128, 24, NSLOT * 128], BF16)
        gB = A([128, 2, SGUD], F32)
        dma("sp", gB, gb_d, writes=["gB"])
        wsb = A([128, 4, 128], BF16)
        wsb16 = A([128, 4, 16], BF16)
        bsb = A([1, 4, 128], BF16)
        vt = [A([128, 4, 128], F32) for _ in range(2)]
        ut = [v_.rearrange("p c f -> p (c f)") for v_ in vt]
        wsf = vt[0]
        bsf = vt[1][0:1]
        dma("sp", wsf, sgu_wsT_d, writes=[("vt", 0)])
        dma("sp", bsf, sgu_bs_d, writes=[("vt", 1)])
        cp("dve", wsb16, wsf[:, :, 0:16], [("vt", 0)], ["wsb16"])
        cp("dve", wsb, wsf, [("vt", 0)], ["wsb"])
        memset("dve", wsb[64:128, :, 0:64], 0.0, ["wsb"])
        cp("dve", bsb, bsf, [("vt", 1)], ["bsb"])
        bst = A([128, NSLOT, 6, 6], F32)
        bmv = A([128, NSLOT, 2], F32)
        brs = A([128, NSLOT], F32)
        lt = vt
        memset("dve", bmv, 1.0, ["bmv"])

        src = sgu_w_in_d.rearrange("(k p) n -> p k n", p=128)
        it = 0
        for pv in range(6):
            wt, wn = wslot()
            wv = wt[:, 0:4096].rearrange("p (k n) -> p k n", k=8)
            wns = wdma(wv, src[:, :, SGUD + pv * 512:SGUD + (pv + 1) * 512], wn, 8)
            for bi, b in enumerate(blocks):
                r = b["rows"]
                ps, pn = psum()
                for kc in range(8):
                    mm(ps[0:r, :], hT[:, kc, b["c0"]:b["c0"] + r], wv[:, kc, :], kc == 0, kc == 7, wns + [hn(st, kc, b["c0"])], [pn])
                bs_ = it % 2
                it += 1
                act(vt[bs_][0:r], ps[0:r, :].rearrange("p (c f) -> p c f", c=4), AF.Gelu_apprx_tanh, [pn], [("vt", bs_)])
                P.add("dve", (lambda o_, i_: (lambda e: e.bn_stats(out=o_, in_=i_)))(bst[0:r, bi, pv, :], vt[bs_][0:r].rearrange("p c f -> p (c f)")),
                      [("vt", bs_)], ["bst"])
                cp("dve", vg[0:r, pv * 4:(pv + 1) * 4, bi * 128:(bi + 1) * 128], vt[bs_][0:r], [("vt", bs_)],
                   [("vv", c) for c in range(pv * 4, pv * 4 + 4)])
        for bi, b in enumerate(blocks):
            r = b["rows"]
            P.add("dve", (lambda o_, i_: (lambda e: e.bn_aggr(out=o_, in_=i_)))(bmv[0:r, bi, :], bst[0:r, bi, :, :].rearrange("p a b -> p (a b)")),
                  ["bst"], ["bmv"])
        ts("dve", brs[:, 0:nb], bmv[:, 0:nb, 1], EPS, None, ALU.add, None, ["bmv"], ["brs"])
        act(brs[:, 0:nb], brs[:, 0:nb], AF.Sqrt, ["brs"], ["brs"])
        P.add("dve", lambda e: e.reciprocal(out=brs[:, 0:nb], in_=brs[:, 0:nb]), ["brs"], ["brs"])
        lts = vt
        snb = A([128, NSLOT], F32)
        stt("dve", snb[:, 0:nb], bmv[:, 0:nb, 0], -1.0, brs[:, 0:nb], ALU.mult, ALU.mult, ["bmv", "brs"], ["snb"])
        li = 0
        for bi, b in enumerate(blocks):
            r = b["rows"]
            for pv in range(6):
                lt_ = lts[li % 2]
                ln_ = ("vt", li % 2)
                li += 1
                vsl = vg[0:r, pv * 4:(pv + 1) * 4, bi * 128:(bi + 1) * 128]
                names = [("vv", c) for c in range(pv * 4, pv * 4 + 4)]
                act(lt_[0:r], vsl, AF.Identity, names + ["snb", "brs"], [ln_], scale=brs[0:r, bi:bi + 1], bias=snb[0:r, bi:bi + 1])
                gsl = gB[0:r, 0, pv * 512:(pv + 1) * 512].rearrange("p (c f) -> p c f", c=4)
                bsl = gB[0:r, 1, pv * 512:(pv + 1) * 512].rearrange("p (c f) -> p c f", c=4)
                tt("dve", lt_[0:r], lt_[0:r], gsl, ALU.mult, [ln_, "gB"], [ln_])
                if b["samp"] is not None:
                    tt("dve", lt_[0:r], lt_[0:r], bsl, ALU.add, [ln_, "gB"], [ln_])
                    dma("sp", sguv_d[b["samp"] * 16:(b["samp"] + 1) * 16, pv * 512:(pv + 1) * 512].rearrange("p (c f) -> p c f", c=4),
                        lt_[0:r], reads=[ln_], out_dma=True)
                    cp("dve", vsl, lt_[0:r], [ln_], names)
                else:
                    tt("dve", vsl, lt_[0:r], bsl, ALU.add, [ln_, "gB"], names)
        for pu in range(6):
            wt, wn = wslot()
            wv = wt[:, 0:4096].rearrange("p (k n) -> p k n", k=8)
            wns = wdma(wv, src[:, :, pu * 512:(pu + 1) * 512], wn, 8)
            for cc in range(4):
                c = pu * 4 + cc
                g = c // 6
                for (c0, n, bl) in st["tiles"]:
                    ps, pn = psum()
                    for kc in range(8):
                        mm(ps[:, 0:n], wv[:, kc, cc * 128:(cc + 1) * 128], hT[:, kc, c0:c0 + n], kc == 0, kc == 7, wns + [hn(st, kc, c0)], [pn])
                    bs_ = it % 2
                    it += 1
                    act(ut[bs_][:, 0:n], ps[:, 0:n], AF.Gelu_apprx_tanh, [pn], [("vt", bs_)])
                    pm, pmn = psum()
                    for bi in bl:
                        b = blocks[bi]
                        r = b["rows"]
                        o0 = b["c0"] - c0
                        wmix = wsb16[0:r, g, 0:r] if b["samp"] is not None else wsb[0:r, g, 0:r]
                        mm(pm[:, o0:o0 + r], vg[0:r, c, bi * 128:(bi + 1) * 128], wmix, True, False,
                           [("vv", c), "wsb", "wsb16"], [pmn])
                        mm(pm[:, o0:o0 + r], ones[0:1, :], bsb[0:1, g, 0:r], False, True, ["ones", "bsb"], [pmn])
                    tt("dve", vg[:, c, c0:c0 + n], pm[:, 0:n], ut[bs_][:, 0:n], ALU.mult, [pmn, ("vt", bs_)], [("vgt", c)])
        for pc in range(4):
            wt, wn = wslot()
            wv = wt[:, 0:6144].rearrange("p (k n) -> p k n", k=24)
            wns = wdma(wv, sgu_w_out_d[:, pc * 256:(pc + 1) * 256].rearrange("(k p) n -> p k n", p=128), wn, 24)
            for o2 in range(2):
                oc = pc * 2 + o2
                for (c0, n, _) in st["tiles"]:
                    ps, pn = psum()
                    for kc in range(24):
                        mm(ps[:, 0:n], wv[:, kc, o2 * 128:(o2 + 1) * 128], vg[:, kc, c0:c0 + n], kc == 0, kc == 23,
                           wns + [("vgt", kc)], [pn])
                    resid_evac(st, ps, pn, oc, c0, n, 1, 16, [])
        A.reset(m0)

    step = [0]

    def go():
        step[0] += 1
        return step[0] <= STOP

    for si, st in enumerate(STS):
        if not go():
            break
        load_x(st)
        norm_mod(st, 0, 0)
        if not go():
            break
        if st["mode"] == "pre":
            retention(st, cspre_d, st["c0"])
            P.barrier()
            continue
        retention(st, cs_d, st["c0"])
        P.barrier()
        last = si == len(STS) - 1
        if last:
            for h in range(HEADS):
                dma("sp", retp_d[h].rearrange("(c p) e -> p c e", p=128), Sst[:, h], reads=[("S", 0), ("S", 1)], out_dma=True)
        if not go():
            break
        norm_mod(st, 0, 1)
        conv_ffn(st, 0, last)
        P.barrier()
        if not go():
            break
        norm_mod(st, 1, 0)
        sgu(st)
        P.barrier()
        if not go():
            break
        norm_mod(st, 1, 1)
        conv_ffn(st, 1, last)
        P.barrier()
        if not go():
            break
        norm_mod(st, 0, 0, final=True)
        P.barrier()
    P.finish("sp")
    P.emit(nc)
    return nc


def _tables(half):
    tabs = np.zeros((128, 1200), np.float32)
    gam = np.array([1.0 - 2.0 ** (-5.0 - h) for h in range(HEADS)], np.float64)
    s = np.arange(128)[:, None]
    l = np.arange(128)[None, :]
    for h in range(HEADS):
        m = gam[h] ** np.abs(l - s) * ((s // 64) <= (l // 64)) / 16.0
        tabs[:, h * 128:(h + 1) * 128] = m
        s16 = np.arange(16)[:, None]
        l16 = np.arange(16)[None, :]
        tabs[0:16, 512 + h * 16:512 + (h + 1) * 16] = gam[h] ** np.abs(l16 - s16) / 16.0
        tabs[:, 576 + h * 128:576 + (h + 1) * 128] = (gam[h] ** (np.arange(128) + 1.0) / 16.0)[None, :]
        tabs[:, 1088 + h * 16:1088 + (h + 1) * 16] = (gam[h] ** (np.arange(16) + 1.0) / 16.0)[None, :]
        tabs[:, 1152 + h] = gam[h] ** (127.0 - np.arange(128))
        tabs[:, 1156 + h] = tabs[:, 1152 + h] * float(half)
        tabs[0:16, 1160 + h] = gam[h] ** (15.0 - np.arange(16))
    tabs[:, 1196] = float(half)
    return tabs


def _cossin(pos):
    inv = np.power(np.float32(10000.0), -np.arange(128, dtype=np.float32) / np.float32(128))
    ang = inv[:, None].astype(np.float32) * pos[None, :].astype(np.float32)
    return np.stack([np.cos(ang), np.sin(ang)], axis=1).astype(np.float32)


_NC_CACHE = {}


def kernel(x_prompt, x_sample, state_ret, state_ffn_conv, c_prompt, c_sample,
           w_ada, b_ada, norm_mix_g, norm_ffn_g, ret_w_in, ret_gn_g, ret_w_out,
           sgu_w_in, sgu_ln_g, sgu_ln_b, sgu_w_s, sgu_b_s, sgu_w_out,
           ffn_w_up, ffn_conv_w, ffn_conv_b, ffn_w_down, final_g):
    f32 = lambda a: np.ascontiguousarray(np.asarray(a, dtype=np.float32))
    x_prompt, x_sample, state_ret, state_ffn_conv = map(f32, (x_prompt, x_sample, state_ret, state_ffn_conv))
    c_prompt, c_sample = f32(c_prompt), f32(c_sample)

    def fm(v, nch):
        return np.asarray(v, np.float32).reshape(nch, 128).T

    vecs = np.zeros((128, 512), np.float32)
    for l in range(2):
        vecs[:, l * 8:(l + 1) * 8] = fm(norm_mix_g[l], 8)
        vecs[:, 16 + l * 8:16 + (l + 1) * 8] = fm(norm_ffn_g[l], 8)
        vecs[:, 56 + l * 48:56 + (l + 1) * 48] = fm(b_ada[l], 48)
        cwl = np.asarray(ffn_conv_w[l], np.float32)
        vecs[:, 152 + l * 132:152 + (l + 1) * 132] = cwl.T.reshape(44, 128, 3).transpose(1, 0, 2).reshape(128, 132)
        vecs[:, 416 + l * 44:416 + (l + 1) * 44] = fm(ffn_conv_b[l], 44)
    vecs[:, 32:40] = fm(final_g, 8)
    vecs[:, 40:56] = fm(ret_gn_g[0], 16)
    gb = np.ascontiguousarray(np.broadcast_to(
        np.stack([np.asarray(sgu_ln_g[0], np.float32), np.asarray(sgu_ln_b[0], np.float32)])[None], (128, 2, SGUD)))
    wsT = np.ascontiguousarray(np.asarray(sgu_w_s[0], np.float32).transpose(2, 0, 1))
    bs = np.ascontiguousarray(np.asarray(sgu_b_s[0], np.float32)[None])
    shared = {
        "vecs": vecs, "gb": gb, "sgu_wsT": wsT, "sgu_bs": bs, "identd": np.eye(128, dtype=np.float32),
        "w_ada": f32(w_ada), "ret_w_in": f32(ret_w_in[0]), "ret_w_out": f32(ret_w_out[0]),
        "sgu_w_in": f32(sgu_w_in[0]), "sgu_w_out": f32(sgu_w_out[0]),
        "ffn_w_up": f32(ffn_w_up), "ffn_w_down": f32(ffn_w_down),
    }
    in_maps = []
    for c in range(8):
        seq, half = c // 2, c % 2
        xT = np.zeros((D, NCOL), np.float32)
        xpre = np.zeros((D, NPRE), np.float32)
        if half:
            xT[:, 0:256] = x_prompt[seq, 1792:2048].T
            xpre[:, :] = x_prompt[seq, 0:1792].T
        xT[:, 256:NR] = x_prompt[seq, half * 2048:(half + 1) * 2048].T
        xT[:, NR:] = x_sample[4 * c:4 * c + 4].reshape(64, D).T
        cT = np.concatenate([c_prompt[seq:seq + 1], c_sample[4 * c:4 * c + 4]], 0).T
        pos = np.concatenate([np.arange(NR, dtype=np.float32) + (1792.0 if half else -256.0),
                              np.tile(np.arange(16, dtype=np.float32) + 1024.0, 4)])
        m = dict(shared)
        m.update({
            "xT": xT, "xpre": xpre, "cT": np.ascontiguousarray(cT.reshape(8, 128, 5).transpose(1, 0, 2).reshape(128, 40)),
            "sret": np.ascontiguousarray(state_ret[0, 4 * c:4 * c + 4]),
            "sconvT": np.ascontiguousarray(state_ffn_conv[:, 4 * c:4 * c + 4].transpose(0, 3, 1, 2).reshape(2, 44, 128, 4, 2).transpose(0, 2, 1, 3, 4).reshape(2, 128, 352)),
            "cs": _cossin(pos), "cspre": _cossin(np.arange(NPRE, dtype=np.float32)),
            "tabs": _tables(half),
        })
        in_maps.append(m)
    if "nc" not in _NC_CACHE:
        _NC_CACHE["nc"] = build_program()
    res = run_bass_kernel_spmd(_NC_CACHE["nc"], in_maps, core_ids=list(range(8)))
    R = res.results
    y_prompt = np.zeros((4, 4096, D), np.float32)
    y_sample = np.zeros((32, 16, D), np.float32)
    ret_p = np.zeros((1, 4, HEADS, DK, DV), np.float32)
    ret_s = np.zeros((1, 32, HEADS, DK, DV), np.float32)
    conv_p = np.zeros((2, 4, 2, 2 * FFN), np.float32)
    conv_s = np.zeros((2, 32, 2, 2 * FFN), np.float32)
    sgu_v = np.zeros((1, 32, 16, SGUD), np.float32)
    for c in range(8):
        seq, half = c // 2, c % 2
        r = R[c]
        yT = r["yT"]
        y_prompt[seq, half * 2048:(half + 1) * 2048] = yT[:, 256:NR].T
        y_sample[4 * c:4 * c + 4] = yT[:, NR:].T.reshape(4, 16, D)
        ret_s[0, 4 * c:4 * c + 4] = r["rets"]
        conv_s[:, 4 * c:4 * c + 4] = r["convs"].reshape(2, 128, 44, 4, 2).transpose(0, 3, 4, 2, 1).reshape(2, 4, 2, 2 * FFN)
        sgu_v[0, 4 * c:4 * c + 4] = r["sguv"].reshape(4, 16, SGUD)
        if half:
            ret_p[0, seq] = r["retp"]
            conv_p[:, seq] = r["convp"].reshape(2, 128, 44, 2).transpose(0, 3, 2, 1).reshape(2, 2, 2 * FFN)
    return (y_prompt, y_sample, ret_p, ret_s, conv_p, conv_s, sgu_v)
```

```python
from contextlib import ExitStack
import numpy as np
import ml_dtypes
import concourse.bass as bass
import concourse.mybir as mybir
from concourse.bass_utils import run_bass_kernel_spmd

F32 = mybir.dt.float32
BF16 = mybir.dt.bfloat16
ALU = mybir.AluOpType
AF = mybir.ActivationFunctionType

ENGS = ("pe", "act", "dve", "pool", "sp")
SEM_LIMIT = 30000
DMA_POOL = 48
SBUF_LIMIT = 229344
SBUF_BASE = 16512

D = 1024
HEADS = 4
DK = 256
DV = 512
FFN = 2816
SGUD = 3072
EPS = 1e-6
NPRE = 1792
NR = 2304
NS = 64
NCOL = NR + NS
WMAX = 832
NSLOT = 10
STOP = 10 ** 9
DBG = set()


class Op:
    __slots__ = ("eng", "fn", "deps", "sig", "token", "is_dma", "out_dma")

    def __init__(self, eng, fn, is_dma):
        self.eng = eng
        self.fn = fn
        self.deps = []
        self.sig = False
        self.token = None
        self.is_dma = is_dma
        self.out_dma = False


class Prog:
    def __init__(self):
        self.ops = {e: [] for e in ENGS}
        self.all = []
        self.lastw = {}
        self.readers = {}
        self.dmas = []
        self.dmas_q = {"hw": [], "sw": []}
        self.bar = []
        self.bar_seen = set(ENGS)
        self.dma_since = []

    def barrier(self):
        lasts = [self.ops[e][-1] for e in ENGS if self.ops[e] and e != "pool"]
        pc = [o for o in self.ops["pool"] if not o.is_dma]
        if pc:
            lasts.append(pc[-1])
        self.bar = lasts + list(self.dma_since)
        self.dma_since = []
        self.bar_seen = {"pool"}

    def add(self, eng, fn, reads=(), writes=(), dma=False, out_dma=False):
        op = Op(eng, fn, dma)
        op.out_dma = out_dma
        deps = []
        if eng not in self.bar_seen:
            deps.extend(self.bar)
            self.bar_seen.add(eng)
        for r in reads:
            w = self.lastw.get(r)
            if w is not None:
                deps.append(w)
        for r in writes:
            w = self.lastw.get(r)
            if w is not None:
                deps.append(w)
            deps.extend(self.readers.get(r, ()))
        for r in reads:
            self.readers.setdefault(r, []).append(op)
        for r in writes:
            self.lastw[r] = op
            self.readers[r] = []
        if dma:
            lst = self.dmas_q["sw" if eng == "pool" else "hw"]
            j = len(lst)
            if j >= DMA_POOL // 2:
                deps.append(lst[j - DMA_POOL // 2])
            lst.append(op)
            self.dmas.append(op)
            if eng != "pool":
                self.dma_since.append(op)
        seen = set()
        for d in deps:
            if d is op or id(d) in seen:
                continue
            seen.add(id(d))
            if d.eng == "pe" and eng == "pe" and not d.is_dma:
                continue
            op.deps.append(d)
        self.ops[eng].append(op)
        self.all.append(op)
        return op

    def finish(self, eng="sp"):
        op = Op(eng, None, False)
        op.deps = [d for d in self.dmas if d.out_dma]
        self.ops[eng].append(op)
        self.all.append(op)

    def emit(self, nc):
        for op in self.all:
            for d in op.deps:
                d.sig = True
        with ExitStack() as st:
            sems = {}
            for e in ENGS:
                n = sum(1 for o in self.ops[e] if o.sig and not o.is_dma)
                k = max(1, (n + SEM_LIMIT - 1) // SEM_LIMIT)
                sems[e] = [st.enter_context(nc.semaphore(f"pg_{e}{i}")) for i in range(k)]
            hp = DMA_POOL // 2
            dsem = {q: [st.enter_context(nc.semaphore(f"d{q}{i}")) for i in range(hp)] for q in ("hw", "sw")}
            for e in ENGS:
                c = 0
                for o in self.ops[e]:
                    if o.is_dma or not o.sig:
                        continue
                    k = c // SEM_LIMIT
                    o.token = (sems[e][k], c - k * SEM_LIMIT + 1)
                    c += 1
            for q in ("hw", "sw"):
                for j, o in enumerate(self.dmas_q[q]):
                    o.token = (dsem[q][j % hp], 16 * (j // hp + 1))
            block = st.enter_context(nc.Block())

            def run(e):
                def body(eng):
                    seen = {}
                    for o in self.ops[e]:
                        for d in o.deps:
                            s, v = d.token
                            if seen.get(id(s), 0) >= v:
                                continue
                            eng.wait_ge(s, v)
                            seen[id(s)] = v
                        if o.fn is None:
                            continue
                        ins = o.fn(eng)
                        if o.is_dma:
                            ins.then_inc(o.token[0], 16)
                        elif o.sig:
                            ins.then_inc(o.token[0], 1)

                return body

            block.tensor(run("pe"))
            block.scalar(run("act"))
            block.vector(run("dve"))
            block.gpsimd(run("pool"))
            block.sync(run("sp"))


class Alloc:
    def __init__(self, nc):
        self.nc = nc
        self.off = SBUF_BASE
        self.n = 0
        self.peak = 0

    def __call__(self, shape, dtype):
        nb = 1
        for s in shape[1:]:
            nb *= s
        nb *= 4 if dtype == F32 else 2
        off = (self.off + 31) // 32 * 32
        t = self.nc.alloc_sbuf_tensor_at(f"t{self.n}", list(shape), dtype, offset=off)
        self.n += 1
        self.off = off + nb
        self.peak = max(self.peak, self.off)
        assert self.off <= SBUF_LIMIT, f"SBUF overflow {self.off}"
        return t.ap()

    def mark(self):
        return self.off

    def reset(self, m):
        self.off = m


def build_program():
    nc = bass.Bass("TRN2", target_bir_lowering=False)
    P = Prog()
    A = Alloc(nc)

    def din(name, shape):
        return nc.dram_tensor(name, list(shape), F32, kind="ExternalInput").ap()

    def dout(name, shape):
        return nc.dram_tensor(name, list(shape), F32, kind="ExternalOutput").ap()

    xT_d = din("xT", (D, NCOL))
    xpre_d = din("xpre", (D, NPRE))
    cT_d = din("cT", (128, 40))
    sret_d = din("sret", (4, HEADS, DK, DV))
    sconv_d = din("sconvT", (2, 128, 352))
    cs_d = din("cs", (128, 2, NCOL))
    cspre_d = din("cspre", (128, 2, NPRE))
    tabs_d = din("tabs", (128, 1200))
    identd_d = din("identd", (128, 128))
    gb_d = din("gb", (128, 2, SGUD))
    vec_d = din("vecs", (128, 512))
    w_ada_d = din("w_ada", (2, D, 6 * D))
    ret_w_in_d = din("ret_w_in", (D, 6 * D))
    ret_w_out_d = din("ret_w_out", (2 * D, D))
    sgu_w_in_d = din("sgu_w_in", (D, 2 * SGUD))
    sgu_w_out_d = din("sgu_w_out", (SGUD, D))
    sgu_wsT_d = din("sgu_wsT", (128, 4, 128))
    sgu_bs_d = din("sgu_bs", (1, 4, 128))
    ffn_w_up_d = din("ffn_w_up", (2, D, 2 * FFN))
    ffn_w_down_d = din("ffn_w_down", (2, FFN, D))

    yT_d = dout("yT", (D, NCOL))
    retp_d = dout("retp", (HEADS, DK, DV))
    rets_d = dout("rets", (4, HEADS, DK, DV))
    convp_d = dout("convp", (2, 128, 88))
    convs_d = dout("convs", (2, 128, 352))
    sguv_d = dout("sguv", (NS, SGUD))

    psb = [nc.alloc_psum_tensor(f"ps{i}", [128, 512], F32).ap() for i in range(6)]
    pT = nc.alloc_psum_tensor("pT", [128, 1024], BF16).ap()
    pT2 = nc.alloc_psum_tensor("pT2", [128, 1024], BF16).ap()
    ps_ctr = [0]

    def psum():
        i = ps_ctr[0] % 6
        ps_ctr[0] += 1
        return psb[i], ("ps", i)

    def mm(out, lhsT, rhs, start, stop, reads, writes):
        P.add("pe", lambda e: e.matmul(out, lhsT=lhsT, rhs=rhs, start=start, stop=stop), reads, writes)

    def tr(out, in_, ident, reads, writes):
        P.add("pe", lambda e: e.transpose(out=out, in_=in_, identity=ident), reads, writes)

    def act(out, in_, func, reads, writes, scale=1.0, bias=0.0):
        if func == AF.Copy and not (isinstance(scale, float) and isinstance(bias, float)):
            func = AF.Identity
        P.add("act", lambda e: e.activation(out=out, in_=in_, func=func, bias=bias, scale=scale), reads, writes)

    def tt(eng, out, in0, in1, op, reads, writes):
        P.add(eng, lambda e: e.tensor_tensor(out=out, in0=in0, in1=in1, op=op), reads, writes)

    def ts(eng, out, in0, s1, s2, op0, op1, reads, writes):
        if s2 is None:
            P.add(eng, lambda e: e.tensor_scalar(out=out, in0=in0, scalar1=s1, scalar2=None, op0=op0), reads, writes)
        else:
            P.add(eng, lambda e: e.tensor_scalar(out=out, in0=in0, scalar1=s1, scalar2=s2, op0=op0, op1=op1), reads, writes)

    def stt(eng, out, in0, scalar, in1, op0, op1, reads, writes):
        P.add(eng, lambda e: e.scalar_tensor_tensor(out=out, in0=in0, scalar=scalar, in1=in1, op0=op0, op1=op1), reads, writes)

    def cp(eng, out, in_, reads, writes):
        if eng == "act":
            act(out, in_, AF.Copy, reads, writes)
        else:
            P.add(eng, lambda e: e.tensor_copy(out=out, in_=in_), reads, writes)

    def dma(q, out, in_, reads=(), writes=(), out_dma=False):
        P.add(q, lambda e: e.dma_start(out=out, in_=in_), reads, writes, dma=True, out_dma=out_dma)

    def wdma(out3, in3, wn, nk, base=0):
        step = 4
        names = []
        for k0 in range(0, nk, step):
            k1 = min(nk, k0 + step)
            nm_ = (wn, base + k0 // step)
            dma("pool", out3[:, k0:k1, :], in3[:, k0:k1, :], writes=[nm_])
            names.append(nm_)
        return names

    def memset(eng, ap, val, writes):
        P.add(eng, lambda e: e.memset(ap, val), (), writes)

    x = A([128, 8, WMAX], F32)
    hT = A([128, 8, WMAX], BF16)
    rstd = A([128, WMAX], F32)
    Sst = A([128, HEADS, 2, DV], F32)
    tabs = A([128, 1200], F32)
    vecs = A([128, 512], F32)
    mod = A([128, 2, 48, 5], F32)
    mul1 = A([128, 2, 2, 8, 5], F32)
    ident = A([128, 128], BF16)
    ones = A([128, 128], BF16)
    flag = tabs[:, 1196:1197]
    hal = A([128, 2, 44, 2], F32)
    n_sq = A([128, 8, 384], BF16)
    n_xr = [A([128, 384], F32) for _ in range(2)]
    NWS = 3
    wsl = [A([128, 6144], BF16) for _ in range(NWS)]
    ws_ctr = [0]

    def wslot():
        i = ws_ctr[0] % NWS
        ws_ctr[0] += 1
        return wsl[i], ("w", i)

    maskT = tabs[:, 0:512].rearrange("p (h l) -> p h l", h=4)
    mask16 = tabs[:, 512:576].rearrange("p (h l) -> p h l", h=4)
    qd = tabs[:, 576:1088].rearrange("p (h l) -> p h l", h=4)
    qd16 = tabs[:, 1088:1152].rearrange("p (h l) -> p h l", h=4)
    kd_main = tabs[:, 1152:1156]
    kd_halo = tabs[:, 1156:1160]
    kd16 = tabs[:, 1160:1164]
    identf = None
    nmg = vecs[:, 0:16].rearrange("p (l c) -> p l c", l=2)
    nfg = vecs[:, 16:32].rearrange("p (l c) -> p l c", l=2)
    fing = vecs[:, 32:40]
    gng = vecs[:, 40:56]
    bada = vecs[:, 56:152].rearrange("p (l c) -> p l c", l=2)
    cw = vecs[:, 152:416].rearrange("p (l c t) -> p l c t", l=2, t=3)
    cb = vecs[:, 416:504].rearrange("p (l c) -> p l c", l=2)

    GAM = [1.0 - 2.0 ** (-5.0 - h) for h in range(HEADS)]

    dma("sp", tabs, tabs_d, writes=["tabs"])
    dma("sp", vecs, vec_d, writes=["vecs"])
    memset("dve", hal, 0.0, [("hal", l_, c_) for l_ in range(2) for c_ in range(44)])
    memset("dve", Sst, 0.0, [("S", 0), ("S", 1)])
    memset("dve", ones, 1.0, ["ones"])

    cTb = A([128, 8, 5], BF16)
    arena0 = A.mark()

    idf = A([128, 128], F32)
    cTs = A([128, 8, 5], F32)
    dma("sp", idf, identd_d, writes=["idf"])
    cp("dve", ident, idf, ["idf"], ["ident"])
    dma("sp", cTs, cT_d.rearrange("p (c s) -> p c s", c=8), writes=["cTs"])
    act(cTb, cTs, AF.Silu, ["cTs"], ["cTb"])
    def adaln_part(l, pcs):
        for pc in pcs:
            wt, wn = wslot()
            wv = wt[:, 0:4096].rearrange("p (k n) -> p k n", k=8)
            wns = wdma(wv, w_ada_d[l, :, pc * 512:(pc + 1) * 512].rearrange("(k p) n -> p k n", p=128), wn, 8)
            ps, pn = psum()
            for oc in range(4):
                for kc in range(8):
                    mm(ps[:, oc * 8:oc * 8 + 5], wv[:, kc, oc * 128:(oc + 1) * 128], cTb[:, kc, :], kc == 0, kc == 7,
                       wns + ["cTb"], [pn])
            for oc in range(4):
                ch = pc * 4 + oc
                act(mod[:, l, ch, :], ps[:, oc * 8:oc * 8 + 5], AF.Identity, [pn, "vecs"], ["mod"], bias=bada[:, l, ch:ch + 1])

    def mul1_part(l, j):
        gv, sc0 = ((nmg, 8), (nfg, 32))[j]
        for c in range(8):
            ts("dve", mul1[:, l, j, c, :], mod[:, l, sc0 + c, :], 1.0, gv[:, l, c:c + 1], ALU.add, ALU.mult,
               ["mod", "vecs"], ["mul1"])

    adaln_part(0, range(0, 4))
    mul1_part(0, 0)
    ada_todo = [(0, pc) for pc in range(4, 12)] + [(1, pc) for pc in range(12)]

    def ada_step(k):
        for _ in range(k):
            if ada_todo:
                l_, pc_ = ada_todo.pop(0)
                adaln_part(l_, [pc_])
                if not ada_todo:
                    mul1_part(0, 1)
                    mul1_part(1, 0)
                    mul1_part(1, 1)
    P.barrier()
    A.reset(arena0)

    def make_st(mode, src, c0, nblk, halo=0, sample=False):
        blocks = []
        for b in range(nblk):
            blocks.append(dict(c0=b * 128, rows=128, samp=None, kd=(kd_halo if (mode == "pre" or b < halo) else kd_main)))
        W = nblk * 128
        tiles = []
        cc = 0
        while cc < W:
            n = min(384, W - cc)
            tiles.append((cc, n, [i for i in range(nblk) if cc <= i * 128 < cc + n]))
            cc += n
        seqs = [(0, W, 0)]
        if sample:
            for s in range(4):
                blocks.append(dict(c0=W + 16 * s, rows=16, samp=s, kd=kd16))
                seqs.append((W + 16 * s, 16, 1 + s))
            tiles.append((W, 64, [nblk + s for s in range(4)]))
            W += 64
        return dict(mode=mode, src=src, c0=c0, blocks=blocks, tiles=tiles, seqs=seqs, W=W, Wp=nblk * 128,
                    sample=sample, halo=halo)

    STS = [
        make_st("pre", xpre_d, 0, 6), make_st("pre", xpre_d, 768, 6), make_st("pre", xpre_d, 1536, 2),
        make_st("full", xT_d, 0, 6, halo=2), make_st("full", xT_d, 768, 6), make_st("full", xT_d, 1536, 6, sample=True),
    ]

    def norm_mod(st, l, j, final=False):
        W = st["W"]
        m0 = A.mark()
        sqs = [n_sq]
        xrs = n_xr
        xi = [0]
        sd = rstd
        for ti_, (c0, n, _) in enumerate(st["tiles"]):
            sq = sqs[0]
            sqn = ("sq", 0)
            for kc in range(8):
                act(sq[:, kc, 0:n], x[:, kc, c0:c0 + n], AF.Square, [("x", kc)], [sqn])
            ps, pn = psum()
            for kc in range(8):
                mm(ps[:, 0:n], ones, sq[:, kc, 0:n], kc == 0, kc == 7, ["ones", sqn], [pn])
            act(sd[:, c0:c0 + n], ps[:, 0:n], AF.Copy, [pn], ["rstd"])
        ts("dve", sd[:, 0:W], sd[:, 0:W], 1.0 / D, EPS, ALU.mult, ALU.add, ["rstd"], ["rstd"])
        act(sd[:, 0:W], sd[:, 0:W], AF.Sqrt, ["rstd"], ["rstd"])
        P.add("dve", lambda e: e.reciprocal(out=rstd[:, 0:W], in_=sd[:, 0:W]), ["rstd"], ["rstd"])
        if final:
            yo = A([128, 8, WMAX], F32)
            for kc in range(8):
                stt("dve", yo[:, kc, 0:W], x[:, kc, 0:W], fing[:, kc:kc + 1], rstd[:, 0:W], ALU.mult, ALU.mult,
                    [("x", kc), "rstd", "vecs"], [("yo", kc)])
                dma("sp", yT_d[kc * 128:(kc + 1) * 128, st["c0"]:st["c0"] + W], yo[:, kc, 0:W], reads=[("yo", kc)], out_dma=True)
        else:
            sh0 = 0 if j == 0 else 24
            for kc in range(8):
                for (c0, n, sq_) in st["seqs"]:
                    for cc in range(c0, c0 + n, 384):
                        nn = min(384, c0 + n - cc)
                        xr = xrs[xi[0] % 2]
                        xrn = ("xr", xi[0] % 2)
                        xi[0] += 1
                        stt("dve", xr[:, 0:nn], x[:, kc, cc:cc + nn], mul1[:, l, j, kc, sq_:sq_ + 1], rstd[:, cc:cc + nn],
                            ALU.mult, ALU.mult, [("x", kc), "rstd", "mul1"], [xrn])
                        act(hT[:, kc, cc:cc + nn], xr[:, 0:nn], AF.Identity, [xrn, "mod"], [hn(st, kc, cc)],
                            bias=mod[:, l, sh0 + kc, sq_:sq_ + 1])
        A.reset(m0)

    def hn(st, kc, col):
        return ("hT", kc, col // 384 if col < st["Wp"] else 9)

    def resid_evac(st, ps, pn, oc, c0, n, l, gch, reads):
        for (s0, sn, sq_) in st["seqs"]:
            a = max(s0, c0)
            b = min(s0 + sn, c0 + n)
            if a >= b:
                continue
            stt("dve", x[:, oc, a:b], ps[:, a - c0:b - c0], mod[:, l, gch + oc, sq_:sq_ + 1], x[:, oc, a:b],
                ALU.mult, ALU.add, [pn, "mod", ("x", oc)] + reads, [("x", oc)])

    def load_x(st):
        W = st["W"]
        for kc in range(8):
            dma("sp", x[:, kc, 0:W], st["src"][kc * 128:(kc + 1) * 128, st["c0"]:st["c0"] + W], writes=[("x", kc)])

    def retention(st, cs_all, cs_c0):
        pre = st["mode"] == "pre"
        W = st["W"]
        blocks = st["blocks"]
        m0 = A.mark()
        cs = A([128, 2, WMAX], F32)
        dma("sp", cs[:, :, 0:W], cs_all[:, :, cs_c0:cs_c0 + W], writes=["cs"])
        kT = A([128, 2, WMAX], BF16)
        vtok = A([128, NSLOT, DV], BF16)
        kdt = A([128, NSLOT, DK], BF16)
        rts = [[A([128, 384], F32) for _ in range(4)] for _ in range(2)]
        rti = [0]
        Sbf = A([128, 2, DV], BF16)
        if not pre:
            qT = A([128, 2, WMAX], BF16)
            qdT = A([128, 2, WMAX], BF16)
            sgT = A([128, 4, WMAX], BF16)
            yT = A([128, 16, WMAX], BF16)
            psb16_l = [A([128, 128], BF16) for _ in range(2)]
            onorm_l = [A([128, DV], BF16) for _ in range(2)]
            bst_l = [A([128, 6], F32) for _ in range(2)]
            bmv_l = [A([128, 2], F32) for _ in range(2)]
            brs_l = [A([128, 1], F32) for _ in range(2)]
            bnb_l = [A([128, 1], F32) for _ in range(2)]
            if st["sample"]:
                S0f = A([128, 2, DV], F32)
                S0b = A([128, 2, DV], BF16)
                Sof = A([128, 2, DV], F32)

        def rotary(p1, p2, pn1, pn2, dst, c0, n, rname):
            cosv = cs[:, 0, c0:c0 + n]
            sinv = cs[:, 1, c0:c0 + n]
            k_ = rti[0] % 2
            rti[0] += 1
            rt = rts[k_]
            tt("dve", rt[0][:, 0:n], p1[:, 0:n], cosv, ALU.mult, [pn1, "cs"], [("rt0", k_)])
            tt("dve", rt[1][:, 0:n], p2[:, 0:n], sinv, ALU.mult, [pn2, "cs"], [("rt1", k_)])
            tt("dve", rt[2][:, 0:n], p1[:, 0:n], sinv, ALU.mult, [pn1, "cs"], [("rt2", k_)])
            tt("dve", rt[3][:, 0:n], p2[:, 0:n], cosv, ALU.mult, [pn2, "cs"], [("rt3", k_)])
            tt("dve", dst[:, 0, c0:c0 + n], rt[0][:, 0:n], rt[1][:, 0:n], ALU.subtract, [("rt0", k_), ("rt1", k_)], [rname])
            tt("dve", dst[:, 1, c0:c0 + n], rt[2][:, 0:n], rt[3][:, 0:n], ALU.add, [("rt2", k_), ("rt3", k_)], [rname])

        for h in range(HEADS):
            wqk, nqk = wslot()
            wqkv = wqk[:, 0:4096].rearrange("p (k n) -> p k n", k=8)
            src = ret_w_in_d.rearrange("(k p) n -> p k n", p=128)
            nqk_l = wdma(wqkv[:, :, 0:256], src[:, :, h * DK:(h + 1) * DK], nqk, 8, 0) + \
                wdma(wqkv[:, :, 256:512], src[:, :, D + h * DK:D + (h + 1) * DK], nqk, 8, 1)
            wv_, nv = wslot()
            wvv = wv_[:, 0:4096].rearrange("p (k n) -> p k n", k=8)
            nv_l = wdma(wvv, src[:, :, 2 * D + h * DV:2 * D + (h + 1) * DV], nv, 8)
            if not pre:
                wg_, ng = wslot()
                wgv = wg_[:, 0:4096].rearrange("p (k n) -> p k n", k=8)
                ng_l = wdma(wgv, src[:, :, 4 * D + h * DV:4 * D + (h + 1) * DV], ng, 8)
            for (c0, n, bl) in st["tiles"]:
                for which in ((1,) if pre else (0, 1)):
                    pp = []
                    for dc in range(2):
                        ps, pn = psum()
                        for kc in range(8):
                            mm(ps[:, 0:n], wqkv[:, kc, which * 256 + dc * 128:which * 256 + (dc + 1) * 128],
                               hT[:, kc, c0:c0 + n], kc == 0, kc == 7, nqk_l + [hn(st, kc, c0)], [pn])
                        pp.append((ps, pn))
                    rotary(pp[0][0], pp[1][0], pp[0][1], pp[1][1], kT if which else qT, c0, n, "kT" if which else "qT")
                for bi in bl:
                    b = blocks[bi]
                    r = b["rows"]
                    ps, pn = psum()
                    for kc in range(8):
                        mm(ps[0:r, :], hT[:, kc, b["c0"]:b["c0"] + r], wvv[:, kc, :], kc == 0, kc == 7, nv_l + [hn(st, kc, b["c0"])], [pn])
                    cp("act", vtok[0:r, bi, :], ps[0:r, :], [pn], [("vtok", bi)])
                if not pre:
                    for ec in range(4):
                        ps, pn = psum()
                        for kc in range(8):
                            mm(ps[:, 0:n], wgv[:, kc, ec * 128:(ec + 1) * 128], hT[:, kc, c0:c0 + n], kc == 0, kc == 7,
                               ng_l + [hn(st, kc, c0)], [pn])
                        act(sgT[:, ec, c0:c0 + n], ps[:, 0:n], AF.Silu, [pn], ["sgT"])
                        ts("dve", sgT[:, ec, c0:c0 + n], sgT[:, ec, c0:c0 + n], gng[:, h * 4 + ec:h * 4 + ec + 1], None, ALU.mult, None,
                           ["sgT", "vecs"], ["sgT"])
            pend_tail = []

            def emit_tail(on_, onn_, r_, c0_, h=h):
                for ec in range(4):
                    tr(pT2[:, ec * 128:ec * 128 + r_], on_[0:r_, ec * 128:(ec + 1) * 128], ident[0:r_, 0:r_],
                       [onn_, "ident"], ["pT2"])
                tt("dve", yT[:, h * 4:(h + 1) * 4, c0_:c0_ + r_], pT2[:, 0:512].rearrange("p (e l) -> p e l", e=4)[:, :, 0:r_],
                   sgT[:, :, c0_:c0_ + r_], ALU.mult, ["pT2", "sgT"], [("yT", h)])

            if not pre:
                for dc_ in range(2):
                    cp("act", Sbf[:, dc_, :], Sst[:, h, dc_, :], [("S", dc_)], [("Sbf", dc_)])
            p_idx = [i for i, b_ in enumerate(blocks) if b_["samp"] is None]
            s_idx = [i for i, b_ in enumerate(blocks) if b_["samp"] is not None]
            order = []
            while p_idx or s_idx:
                if p_idx:
                    order.append(p_idx.pop(0))
                if s_idx:
                    order.append(s_idx.pop(0))
            for oi, bi in enumerate(order):
                b = blocks[bi]
                r = b["rows"]
                c0 = b["c0"]
                samp = b["samp"]
                L = 16 if samp is not None else 128
                if not pre:
                    rb = oi % 2
                    psb16, onorm, bst, bmv, brs, bnb = psb16_l[rb], onorm_l[rb], bst_l[rb], bmv_l[rb], brs_l[rb], bnb_l[rb]
                    nm = lambda base: (base, rb)
                if samp is not None and "noR" in DBG:
                    continue
                for dc in range(2):
                    tr(pT[0:r, dc * 128:(dc + 1) * 128], kT[:, dc, c0:c0 + r], ident, ["kT", "ident"], ["pT"])
                for dc in range(2):
                    act(kdt[0:r, bi, dc * 128:(dc + 1) * 128], pT[0:r, dc * 128:(dc + 1) * 128], AF.Copy, ["pT", "tabs"],
                        [("kdt", bi)], scale=b["kd"][0:r, h:h + 1])
                if samp is not None and "noD" not in DBG:
                    dma("sp", S0f, sret_d[samp, h].rearrange("(c p) e -> p c e", p=128), writes=["S0f"])
                    cp("act", S0b, S0f, ["S0f"], ["S0b"])
                    Sb_, Sbn = S0b, "S0b"
                elif samp is not None:
                    Sb_, Sbn = S0b, "S0b"
                else:
                    Sb_, Sbn = Sbf, "Sbf"
                do_o = not pre and not (samp is not None and "noO" in DBG)
                if do_o:
                    ps, pn = psum()
                    for dc in range(2):
                        mm(ps[0:r, 0:r], kT[:, dc, c0:c0 + r], qT[:, dc, c0:c0 + r], dc == 0, dc == 1, ["kT", "qT"], [pn])
                    mk = mask16[0:r, h, 0:r] if samp is not None else maskT[0:r, h, 0:r]
                    tt("dve", psb16[0:r, 0:r], ps[0:r, 0:r], mk, ALU.mult, [pn, "tabs"], [nm("psb16")])
                    qdv = (qd16 if samp is not None else qd)[:, h, 0:r]
                    for dc in range(2):
                        tt("dve", qdT[:, dc, c0:c0 + r], qT[:, dc, c0:c0 + r], qdv, ALU.mult, ["qT", "tabs"], [("qdT", bi)])
                    po, pon = psum()
                    mm(po[0:r, :], psb16[0:r, 0:r], vtok[0:r, bi, :], True, False, [nm("psb16"), ("vtok", bi)], [pon])
                    for dc in range(2):
                        mm(po[0:r, :], qdT[:, dc, c0:c0 + r], Sb_[:, dc, :], False, dc == 1,
                           [("qdT", bi), (Sbn, dc) if Sbn == "Sbf" else Sbn], [pon])
                if samp is not None and "noS" in DBG:
                    continue
                dec = float(np.float32(GAM[h]) ** np.float32(L))
                for dc in range(2):
                    ps, pn = psum()
                    mm(ps[:, :], kdt[0:r, bi, dc * 128:(dc + 1) * 128], vtok[0:r, bi, :], True, True,
                       [("kdt", bi), ("vtok", bi)], [pn])
                    if samp is not None:
                        stt("dve", Sof[:, dc, :], S0f[:, dc, :], dec, ps[:, :], ALU.mult, ALU.add, ["S0f", pn], ["Sof"])
                    else:
                        stt("dve", Sst[:, h, dc, :], Sst[:, h, dc, :], dec, ps[:, :], ALU.mult, ALU.add, [("S", dc), pn], [("S", dc)])
                        if not pre:
                            cp("act", Sbf[:, dc, :], Sst[:, h, dc, :], [("S", dc)], [("Sbf", dc)])
                if samp is not None and "noD" in DBG:
                    pass
                elif samp is not None:
                    dma("sp", rets_d[samp, h].rearrange("(c p) e -> p c e", p=128), Sof, reads=["Sof"], out_dma=True)
                if do_o:
                    P.add("dve", (lambda o_, i_: (lambda e: e.bn_stats(out=o_, in_=i_)))(bst[0:r, :], po[0:r, :]), [pon], [nm("bst")])
                    P.add("dve", (lambda o_, i_: (lambda e: e.bn_aggr(out=o_, in_=i_)))(bmv[0:r, :], bst[0:r, :]), [nm("bst")], [nm("bmv")])
                    ts("dve", brs[0:r, :], bmv[0:r, 1:2], EPS, None, ALU.add, None, [nm("bmv")], [nm("brs")])
                    act(brs[0:r, :], brs[0:r, :], AF.Sqrt, [nm("brs")], [nm("brs")])
                    P.add("dve", (lambda o_, i_: (lambda e: e.reciprocal(out=o_, in_=i_)))(brs[0:r, :], brs[0:r, :]), [nm("brs")], [nm("brs")])
                    stt("dve", bnb[0:r, :], bmv[0:r, 0:1], -1.0, brs[0:r, 0:1], ALU.mult, ALU.mult, [nm("bmv"), nm("brs")], [nm("bnb")])
                    act(onorm[0:r, :], po[0:r, :], AF.Identity, [pon, nm("brs"), nm("bnb")], [nm("onorm")],
                        scale=brs[0:r, 0:1], bias=bnb[0:r, 0:1])
                    pend_tail.append((onorm, nm("onorm"), r, c0))
                    if len(pend_tail) > 1:
                        emit_tail(*pend_tail.pop(0))
            while pend_tail:
                emit_tail(*pend_tail.pop(0))
            ada_step(1 if pre else 2)
        if not pre:
            ada_step(100)
            for pc in range(4):
                wo, no = wslot()
                wov = wo[:, 0:4096].rearrange("p (k n) -> p k n", k=16)
                no_l = wdma(wov, ret_w_out_d[:, pc * 256:(pc + 1) * 256].rearrange("(k p) n -> p k n", p=128), no, 16)
                for o2 in range(2):
                    oc = pc * 2 + o2
                    for (c0, n, _) in st["tiles"]:
                        ps, pn = psum()
                        for kc in range(16):
                            mm(ps[:, 0:n], wov[:, kc, o2 * 128:(o2 + 1) * 128], yT[:, kc, c0:c0 + n], kc == 0, kc == 15,
                               no_l + [("yT", kc // 4)], [pn])
                        resid_evac(st, ps, pn, oc, c0, n, 0, 16, [])
        A.reset(m0)

    def conv_ffn(st, l, last):
        W = st["W"]
        m0 = A.mark()
        mT = A([128, 22, WMAX], BF16)
        ab = [[A([128, 386], F32) for _ in range(2)] for _ in range(2)]
        acc = [[A([128, 384], F32) for _ in range(2)] for _ in range(2)]
        sgb = [A([128, 384], F32) for _ in range(2)]
        if st["sample"]:
            scs = A([128, 44, 4, 2], F32)
            dma("sp", scs, sconv_d[l].rearrange("p (c s r) -> p c s r", c=44, s=4), writes=["scs"])
            abs_ = [A([128, 4, 18], F32) for _ in range(2)]
            accs = [A([128, 4, 16], F32) for _ in range(2)]
            sgs = A([128, 4, 16], F32)
            csts = A([128, 44, 4, 2], F32)
        if last:
            cstp = A([128, 44, 2], F32)
        if st["halo"]:
            ts("dve", hT[:, :, 254:256], hT[:, :, 254:256], flag, None, ALU.mult, None, [("hT", k, 0) for k in range(8)] + ["tabs"],
               [("hT", k, 0) for k in range(8)])
        it = 0
        for pj in range(11):
            wt, wn = wslot()
            wv = wt[:, 0:4096].rearrange("p (k n) -> p k n", k=8)
            src = ffn_w_up_d[l].rearrange("(k p) n -> p k n", p=128)
            wns = wdma(wv[:, :, 0:256], src[:, :, pj * 256:(pj + 1) * 256], wn, 8, 0) + \
                wdma(wv[:, :, 256:512], src[:, :, FFN + pj * 256:FFN + (pj + 1) * 256], wn, 8, 1)
            for jj in range(2):
                j = pj * 2 + jj
                chs = (j, 22 + j)
                for ti, (c0, n, bl) in enumerate(st["tiles"]):
                    is_s = st["sample"] and ti == len(st["tiles"]) - 1
                    pps = []
                    for gv in range(2):
                        ps, pn = psum()
                        for kc in range(8):
                            mm(ps[:, 0:n], wv[:, kc, gv * 256 + jj * 128:gv * 256 + (jj + 1) * 128], hT[:, kc, c0:c0 + n],
                               kc == 0, kc == 7, wns + [hn(st, kc, c0)], [pn])
                        pps.append((ps, pn))
                    bsel = it % 2
                    it += 1
                    nprompt = len(st["tiles"]) - (1 if st["sample"] else 0)
                    for gv in range(2):
                        ps, pn = pps[gv]
                        ch = chs[gv]
                        if not is_s:
                            a_ = ab[bsel][gv]
                            an = ("ab", bsel, gv)
                            ahn = ("abh", bsel, gv)
                            ac_ = acc[bsel][gv]
                            acn = ("acc", bsel, gv)
                            if ti == 0:
                                cp("act", a_[:, 0:2], hal[:, l, ch, :], [("hal", l, ch)], [ahn])
                            act(a_[:, 2:2 + n], ps[:, 0:n], AF.Copy, [pn], [an])
                            if ti + 1 < nprompt:
                                cp("act", ab[1 - bsel][gv][:, 0:2], ps[:, n - 2:n], [pn], [("abh", 1 - bsel, gv)])
                            else:
                                cp("act", hal[:, l, ch, :], ps[:, n - 2:n], [pn], [("hal", l, ch)])
                                if last:
                                    cp("act", cstp[:, ch, :], ps[:, n - 2:n], [pn], ["cstp"])
                            act(ac_[:, 0:n], ps[:, 0:n], AF.Identity, [pn, "vecs"], [acn], scale=cw[:, l, ch, 2:3], bias=cb[:, l, ch:ch + 1])
                            stt("dve", ac_[:, 0:n], a_[:, 1:1 + n], cw[:, l, ch, 1:2], ac_[:, 0:n], ALU.mult, ALU.add, [an, ahn, acn, "vecs"], [acn])
                            stt("dve", ac_[:, 0:n], a_[:, 0:n], cw[:, l, ch, 0:1], ac_[:, 0:n], ALU.mult, ALU.add, [an, ahn, acn, "vecs"], [acn])
                        else:
                            a_ = abs_[gv]
                            an = ("abs", gv)
                            ac_ = accs[gv]
                            acn = ("accs", gv)
                            cp("act", a_[:, :, 0:2], scs[:, ch, :, :], ["scs"], [an])
                            act(a_[:, :, 2:18], ps[:, 0:64].rearrange("p (s t) -> p s t", s=4), AF.Copy, [pn], [an])
                            act(ac_, ps[:, 0:64].rearrange("p (s t) -> p s t", s=4), AF.Identity, [pn, "vecs"], [acn],
                                scale=cw[:, l, ch, 2:3], bias=cb[:, l, ch:ch + 1])
                            stt("dve", ac_, a_[:, :, 1:17], cw[:, l, ch, 1:2], ac_, ALU.mult, ALU.add, [an, acn, "vecs"], [acn])
                            stt("dve", ac_, a_[:, :, 0:16], cw[:, l, ch, 0:1], ac_, ALU.mult, ALU.add, [an, acn, "vecs"], [acn])
                            cp("act", csts[:, ch, :, :], a_[:, :, 16:18], [an], ["csts"])
                    if not is_s:
                        act(sgb[bsel][:, 0:n], acc[bsel][0][:, 0:n], AF.Silu, [("acc", bsel, 0)], [("sgb", bsel)])
                        tt("dve", mT[:, j, c0:c0 + n], sgb[bsel][:, 0:n], acc[bsel][1][:, 0:n], ALU.mult,
                           [("sgb", bsel), ("acc", bsel, 1)], [("mT", j)])
                    else:
                        act(sgs, accs[0], AF.Silu, [("accs", 0)], ["sgs"])
                        tt("dve", mT[:, j, c0:c0 + 64].rearrange("p (s t) -> p s t", s=4), sgs, accs[1], ALU.mult,
                           ["sgs", ("accs", 1)], [("mT", j)])
        if st["sample"]:
            dma("sp", convs_d[l].rearrange("p (c s r) -> p c s r", c=44, s=4), csts, reads=["csts"], out_dma=True)
        if last:
            dma("sp", convp_d[l].rearrange("p (c r) -> p c r", c=44), cstp, reads=["cstp"], out_dma=True)
        for pc in range(4):
            wt, wn = wslot()
            wv = wt[:, 0:5632].rearrange("p (k n) -> p k n", k=22)
            wns = wdma(wv, ffn_w_down_d[l][:, pc * 256:(pc + 1) * 256].rearrange("(k p) n -> p k n", p=128), wn, 22)
            for o2 in range(2):
                oc = pc * 2 + o2
                for (c0, n, _) in st["tiles"]:
                    ps, pn = psum()
                    for kc in range(22):
                        mm(ps[:, 0:n], wv[:, kc, o2 * 128:(o2 + 1) * 128], mT[:, kc, c0:c0 + n], kc == 0, kc == 21,
                           wns + [("mT", kc)], [pn])
                    resid_evac(st, ps, pn, oc, c0, n, l, 40, [])
        A.reset(m0)

    def sgu(st):
        W = st["W"]
        blocks = st["blocks"]
        nb = len(blocks)
        m0 = A.mark()
        vg = A([128, 24, NSLOT * 128], BF16)
        gB = A([128, 2, SGUD], F32)
        dma("sp", gB, gb_d, writes=["gB"])
        wsb = A([128, 4, 128], BF16)
        wsb16 = A([128, 4, 16], BF16)
        bsb = A([1, 4, 128], BF16)
        vt = [A([128, 4, 128], F32) for _ in range(2)]
        ut = [v_.rearrange("p c f -> p (c f)") for v_ in vt]
        wsf = vt[0]
        bsf = vt[1][0:1]
        dma("sp", wsf, sgu_wsT_d, writes=[("vt", 0)])
        dma("sp", bsf, sgu_bs_d, writes=[("vt", 1)])
        cp("dve", wsb16, wsf[:, :, 0:16], [("vt", 0)], ["wsb16"])
        cp("dve", wsb, wsf, [("vt", 0)], ["wsb"])
        memset("dve", wsb[64:128, :, 0:64], 0.0, ["wsb"])
        cp("dve", bsb, bsf, [("vt", 1)], ["bsb"])
        bst = A([128, NSLOT, 6, 6], F32)
        bmv = A([128, NSLOT, 2], F32)
        brs = A([128, NSLOT], F32)
        lt = vt
        memset("dve", bmv, 1.0, ["bmv"])

        src = sgu_w_in_d.rearrange("(k p) n -> p k n", p=128)
        it = 0
        for pv in range(6):
            wt, wn = wslot()
            wv = wt[:, 0:4096].rearrange("p (k n) -> p k n", k=8)
            wns = wdma(wv, src[:, :, SGUD + pv * 512:SGUD + (pv + 1) * 512], wn, 8)
            for bi, b in enumerate(blocks):
                r = b["rows"]
                ps, pn = psum()
                for kc in range(8):
                    mm(ps[0:r, :], hT[:, kc, b["c0"]:b["c0"] + r], wv[:, kc, :], kc == 0, kc == 7, wns + [hn(st, kc, b["c0"])], [pn])
                bs_ = it % 2
                it += 1
                act(vt[bs_][0:r], ps[0:r, :].rearrange("p (c f) -> p c f", c=4), AF.Gelu_apprx_tanh, [pn], [("vt", bs_)])
                P.add("dve", (lambda o_, i_: (lambda e: e.bn_stats(out=o_, in_=i_)))(bst[0:r, bi, pv, :], vt[bs_][0:r].rearrange("p c f -> p (c f)")),
                      [("vt", bs_)], ["bst"])
                cp("dve", vg[0:r, pv * 4:(pv + 1) * 4, bi * 128:(bi + 1) * 128], vt[bs_][0:r], [("vt", bs_)],
                   [("vv", c) for c in range(pv * 4, pv * 4 + 4)])
        for bi, b in enumerate(blocks):
            r = b["rows"]
            P.add("dve", (lambda o_, i_: (lambda e: e.bn_aggr(out=o_, in_=i_)))(bmv[0:r, bi, :], bst[0:r, bi, :, :].rearrange("p a b -> p (a b)")),
                  ["bst"], ["bmv"])
        ts("dve", brs[:, 0:nb], bmv[:, 0:nb, 1], EPS, None, ALU.add, None, ["bmv"], ["brs"])
        act(brs[:, 0:nb], brs[:, 0:nb], AF.Sqrt, ["brs"], ["brs"])
        P.add("dve", lambda e: e.reciprocal(out=brs[:, 0:nb], in_=brs[:, 0:nb]), ["brs"], ["brs"])
        lts = vt
        snb = A([128, NSLOT], F32)
        stt("dve", snb[:, 0:nb], bmv[:, 0:nb, 0], -1.0, brs[:, 0:nb], ALU.mult, ALU.mult, ["bmv", "brs"], ["snb"])
        li = 0
        for bi, b in enumerate(blocks):
            r = b["rows"]
            for pv in range(6):
                lt_ = lts[li % 2]
                ln_ = ("vt", li % 2)
                li += 1
                vsl = vg[0:r, pv * 4:(pv + 1) * 4, bi * 128:(bi + 1) * 128]
                names = [("vv", c) for c in range(pv * 4, pv * 4 + 4)]
                act(lt_[0:r], vsl, AF.Identity, names + ["snb", "brs"], [ln_], scale=brs[0:r, bi:bi + 1], bias=snb[0:r, bi:bi + 1])
                gsl = gB[0:r, 0, pv * 512:(pv + 1) * 512].rearrange("p (c f) -> p c f", c=4)
                bsl = gB[0:r, 1, pv * 512:(pv + 1) * 512].rearrange("p (c f) -> p c f", c=4)
                tt("dve", lt_[0:r], lt_[0:r], gsl, ALU.mult, [ln_, "gB"], [ln_])
                if b["samp"] is not None:
                    tt("dve", lt_[0:r], lt_[0:r], bsl, ALU.add, [ln_, "gB"], [ln_])
                    dma("sp", sguv_d[b["samp"] * 16:(b["samp"] + 1) * 16, pv * 512:(pv + 1) * 512].rearrange("p (c f) -> p c f", c=4),
                        lt_[0:r], reads=[ln_], out_dma=True)
                    cp("dve", vsl, lt_[0:r], [ln_], names)
                else:
                    tt("dve", vsl, lt_[0:r], bsl, ALU.add, [ln_, "gB"], names)
        for pu in range(6):
            wt, wn = wslot()
            wv = wt[:, 0:4096].rearrange("p (k n) -> p k n", k=8)
            wns = wdma(wv, src[:, :, pu * 512:(pu + 1) * 512], wn, 8)
            for cc in range(4):
                c = pu * 4 + cc
                g = c // 6
                for (c0, n, bl) in st["tiles"]:
                    ps, pn = psum()
                    for kc in range(8):
                        mm(ps[:, 0:n], wv[:, kc, cc * 128:(cc + 1) * 128], hT[:, kc, c0:c0 + n], kc == 0, kc == 7, wns + [hn(st, kc, c0)], [pn])
                    bs_ = it % 2
                    it += 1
                    act(ut[bs_][:, 0:n], ps[:, 0:n], AF.Gelu_apprx_tanh, [pn], [("vt", bs_)])
                    pm, pmn = psum()
                    for bi in bl:
                        b = blocks[bi]
                        r = b["rows"]
                        o0 = b["c0"] - c0
                        wmix = wsb16[0:r, g, 0:r] if b["samp"] is not None else wsb[0:r, g, 0:r]
                        mm(pm[:, o0:o0 + r], vg[0:r, c, bi * 128:(bi + 1) * 128], wmix, True, False,
                           [("vv", c), "wsb", "wsb16"], [pmn])
                        mm(pm[:, o0:o0 + r], ones[0:1, :], bsb[0:1, g, 0:r], False, True, ["ones", "bsb"], [pmn])
                    tt("dve", vg[:, c, c0:c0 + n], pm[:, 0:n], ut[bs_][:, 0:n], ALU.mult, [pmn, ("vt", bs_)], [("vgt", c)])
        for pc in range(4):
            wt, wn = wslot()
            wv = wt[:, 0:6144].rearrange("p (k n) -> p k n", k=24)
            wns = wdma(wv, sgu_w_out_d[:, pc * 256:(pc + 1) * 256].rearrange("(k p) n -> p k n", p=128), wn, 24)
            for o2 in range(2):
                oc = pc * 2 + o2
                for (c0, n, _) in st["tiles"]:
                    ps, pn = psum()
                    for kc in range(24):
                        mm(ps[:, 0:n], wv[:, kc, o2 * 128:(o2 + 1) * 128], vg[:, kc, c0:c0 + n], kc == 0, kc == 23,
                           wns + [("vgt", kc)], [pn])
                    resid_evac(st, ps, pn, oc, c0, n, 1, 16, [])
        A.reset(m0)

    step = [0]

    def go():
        step[0] += 1
        return step[0] <= STOP

    for si, st in enumerate(STS):
        if not go():
            break
        load_x(st)
        norm_mod(st, 0, 0)
        if not go():
            break
        if st["mode"] == "pre":
            retention(st, cspre_d, st["c0"])
            P.barrier()
            continue
        retention(st, cs_d, st["c0"])
        P.barrier()
        last = si == len(STS) - 1
        if last:
            for h in range(HEADS):
                dma("sp", retp_d[h].rearrange("(c p) e -> p c e", p=128), Sst[:, h], reads=[("S", 0), ("S", 1)], out_dma=True)
        if not go():
            break
        norm_mod(st, 0, 1)
        conv_ffn(st, 0, last)
        P.barrier()
        if not go():
            break
        norm_mod(st, 1, 0)
        sgu(st)
        P.barrier()
        if not go():
            break
        norm_mod(st, 1, 1)
        conv_ffn(st, 1, last)
        P.barrier()
        if not go():
            break
        norm_mod(st, 0, 0, final=True)
        P.barrier()
    P.finish("sp")
    P.emit(nc)
    return nc


def _tables(half):
    tabs = np.zeros((128, 1200), np.float32)
    gam = np.array([1.0 - 2.0 ** (-5.0 - h) for h in range(HEADS)], np.float64)
    s = np.arange(128)[:, None]
    l = np.arange(128)[None, :]
    for h in range(HEADS):
        m = gam[h] ** np.abs(l - s) * ((s // 64) <= (l // 64)) / 16.0
        tabs[:, h * 128:(h + 1) * 128] = m
        s16 = np.arange(16)[:, None]
        l16 = np.arange(16)[None, :]
        tabs[0:16, 512 + h * 16:512 + (h + 1) * 16] = gam[h] ** np.abs(l16 - s16) / 16.0
        tabs[:, 576 + h * 128:576 + (h + 1) * 128] = (gam[h] ** (np.arange(128) + 1.0) / 16.0)[None, :]
        tabs[:, 1088 + h * 16:1088 + (h + 1) * 16] = (gam[h] ** (np.arange(16) + 1.0) / 16.0)[None, :]
        tabs[:, 1152 + h] = gam[h] ** (127.0 - np.arange(128))
        tabs[:, 1156 + h] = tabs[:, 1152 + h] * float(half)
        tabs[0:16, 1160 + h] = gam[h] ** (15.0 - np.arange(16))
    tabs[:, 1196] = float(half)
    return tabs


def _cossin(pos):
    inv = np.power(np.float32(10000.0), -np.arange(128, dtype=np.float32) / np.float32(128))
    ang = inv[:, None].astype(np.float32) * pos[None, :].astype(np.float32)
    return np.stack([np.cos(ang), np.sin(ang)], axis=1).astype(np.float32)


_NC_CACHE = {}


def kernel(x_prompt, x_sample, state_ret, state_ffn_conv, c_prompt, c_sample,
           w_ada, b_ada, norm_mix_g, norm_ffn_g, ret_w_in, ret_gn_g, ret_w_out,
           sgu_w_in, sgu_ln_g, sgu_ln_b, sgu_w_s, sgu_b_s, sgu_w_out,
           ffn_w_up, ffn_conv_w, ffn_conv_b, ffn_w_down, final_g):
    f32 = lambda a: np.ascontiguousarray(np.asarray(a, dtype=np.float32))
    x_prompt, x_sample, state_ret, state_ffn_conv = map(f32, (x_prompt, x_sample, state_ret, state_ffn_conv))
    c_prompt, c_sample = f32(c_prompt), f32(c_sample)

    def fm(v, nch):
        return np.asarray(v, np.float32).reshape(nch, 128).T

    vecs = np.zeros((128, 512), np.float32)
    for l in range(2):
        vecs[:, l * 8:(l + 1) * 8] = fm(norm_mix_g[l], 8)
        vecs[:, 16 + l * 8:16 + (l + 1) * 8] = fm(norm_ffn_g[l], 8)
        vecs[:, 56 + l * 48:56 + (l + 1) * 48] = fm(b_ada[l], 48)
        cwl = np.asarray(ffn_conv_w[l], np.float32)
        vecs[:, 152 + l * 132:152 + (l + 1) * 132] = cwl.T.reshape(44, 128, 3).transpose(1, 0, 2).reshape(128, 132)
        vecs[:, 416 + l * 44:416 + (l + 1) * 44] = fm(ffn_conv_b[l], 44)
    vecs[:, 32:40] = fm(final_g, 8)
    vecs[:, 40:56] = fm(ret_gn_g[0], 16)
    gb = np.ascontiguousarray(np.broadcast_to(
        np.stack([np.asarray(sgu_ln_g[0], np.float32), np.asarray(sgu_ln_b[0], np.float32)])[None], (128, 2, SGUD)))
    wsT = np.ascontiguousarray(np.asarray(sgu_w_s[0], np.float32).transpose(2, 0, 1))
    bs = np.ascontiguousarray(np.asarray(sgu_b_s[0], np.float32)[None])
    shared = {
        "vecs": vecs, "gb": gb, "sgu_wsT": wsT, "sgu_bs": bs, "identd": np.eye(128, dtype=np.float32),
        "w_ada": f32(w_ada), "ret_w_in": f32(ret_w_in[0]), "ret_w_out": f32(ret_w_out[0]),
        "sgu_w_in": f32(sgu_w_in[0]), "sgu_w_out": f32(sgu_w_out[0]),
        "ffn_w_up": f32(ffn_w_up), "ffn_w_down": f32(ffn_w_down),
    }
    in_maps = []
    for c in range(8):
        seq, half = c // 2, c % 2
        xT = np.zeros((D, NCOL), np.float32)
        xpre = np.zeros((D, NPRE), np.float32)
        if half:
            xT[:, 0:256] = x_prompt[seq, 1792:2048].T
            xpre[:, :] = x_prompt[seq, 0:1792].T
        xT[:, 256:NR] = x_prompt[seq, half * 2048:(half + 1) * 2048].T
        xT[:, NR:] = x_sample[4 * c:4 * c + 4].reshape(64, D).T
        cT = np.concatenate([c_prompt[seq:seq + 1], c_sample[4 * c:4 * c + 4]], 0).T
        pos = np.concatenate([np.arange(NR, dtype=np.float32) + (1792.0 if half else -256.0),
                              np.tile(np.arange(16, dtype=np.float32) + 1024.0, 4)])
        m = dict(shared)
        m.update({
            "xT": xT, "xpre": xpre, "cT": np.ascontiguousarray(cT.reshape(8, 128, 5).transpose(1, 0, 2).reshape(128, 40)),
            "sret": np.ascontiguousarray(state_ret[0, 4 * c:4 * c + 4]),
            "sconvT": np.ascontiguousarray(state_ffn_conv[:, 4 * c:4 * c + 4].transpose(0, 3, 1, 2).reshape(2, 44, 128, 4, 2).transpose(0, 2, 1, 3, 4).reshape(2, 128, 352)),
            "cs": _cossin(pos), "cspre": _cossin(np.arange(NPRE, dtype=np.float32)),
            "tabs": _tables(half),
        })
        in_maps.append(m)
    if "nc" not in _NC_CACHE:
        _NC_CACHE["nc"] = build_program()
    res = run_bass_kernel_spmd(_NC_CACHE["nc"], in_maps, core_ids=list(range(8)))
    R = res.results
    y_prompt = np.zeros((4, 4096, D), np.float32)
    y_sample = np.zeros((32, 16, D), np.float32)
    ret_p = np.zeros((1, 4, HEADS, DK, DV), np.float32)
    ret_s = np.zeros((1, 32, HEADS, DK, DV), np.float32)
    conv_p = np.zeros((2, 4, 2, 2 * FFN), np.float32)
    conv_s = np.zeros((2, 32, 2, 2 * FFN), np.float32)
    sgu_v = np.zeros((1, 32, 16, SGUD), np.float32)
    for c in range(8):
        seq, half = c // 2, c % 2
        r = R[c]
        yT = r["yT"]
        y_prompt[seq, half * 2048:(half + 1) * 2048] = yT[:, 256:NR].T
        y_sample[4 * c:4 * c + 4] = yT[:, NR:].T.reshape(4, 16, D)
        ret_s[0, 4 * c:4 * c + 4] = r["rets"]
        conv_s[:, 4 * c:4 * c + 4] = r["convs"].reshape(2, 128, 44, 4, 2).transpose(0, 3, 4, 2, 1).reshape(2, 4, 2, 2 * FFN)
        sgu_v[0, 4 * c:4 * c + 4] = r["sguv"].reshape(4, 16, SGUD)
        if half:
            ret_p[0, seq] = r["retp"]
            conv_p[:, seq] = r["convp"].reshape(2, 128, 44, 2).transpose(0, 3, 2, 1).reshape(2, 2, 2 * FFN)
    return (y_prompt, y_sample, ret_p, ret_s, conv_p, conv_s, sgu_v)
```
